# Optimizing a Trainium2 kernel written in Bass

```python
import jax, jax.numpy as jnp
from jax import lax
import numpy as np

D_MODEL = 1024
BATCH = 4
SEQ = 4096
DEPTH = 1

N_META = 16
ATTN_BLOCK = 128
META_PAD = ATTN_BLOCK - N_META
PREFIX = ATTN_BLOCK
FOX_HEADS = 8
FOX_HEAD_DIM = 64
FOX_WIDTH = FOX_HEADS * FOX_HEAD_DIM
DN_HEADS = 4
DN_HEAD_DIM = 128
DN_WIDTH = DN_HEADS * DN_HEAD_DIM
DN_CHUNK = 64
CONV_WIDTH = 4
D_FF = -(-8 * D_MODEL // (3 * 256)) * 256
EPS = 1e-6
NEG_INF = -1e30
IN_SPLITS = (FOX_WIDTH, FOX_WIDTH, FOX_WIDTH, FOX_HEADS,
             DN_WIDTH, DN_WIDTH, DN_WIDTH, DN_HEADS, DN_HEADS,
             DN_WIDTH,
             D_MODEL, D_MODEL)
D_IN = sum(IN_SPLITS)

kernel_name = "fox_gdn_gated_hybrid_block"


def rmsnorm(x, w):
    xf = x.astype(jnp.float32)
    y = xf * lax.rsqrt(jnp.mean(xf * xf, axis=-1, keepdims=True) + EPS)
    return (y * w.astype(jnp.float32)).astype(x.dtype)


def l2norm(x):
    return x * lax.rsqrt(jnp.sum(x * x, axis=-1, keepdims=True) + EPS)


def split_columns(t, sizes):
    out, start = [], 0
    for s in sizes:
        out.append(t[..., start:start + s])
        start += s
    return out


def causal_short_conv(u, w):
    K = w.shape[0]
    L = u.shape[1]
    up = jnp.pad(u, ((0, 0), (K - 1, 0), (0, 0)))
    return sum(up[:, i:i + L, :] * w[i] for i in range(K))


def fox_attention(q, k, v, log_f, valid):
    B, L, H, dh = q.shape
    nb = L // ATTN_BLOCK
    qf = q.astype(jnp.float32) * (dh ** -0.5)
    kf = k.astype(jnp.float32)
    vf = v.astype(jnp.float32)
    c = jnp.cumsum(log_f, axis=1)
    c_keys = c.transpose(0, 2, 1)
    kpos = jnp.arange(L)
    q_blocks = qf.reshape(B, nb, ATTN_BLOCK, H, dh).transpose(1, 0, 2, 3, 4)
    c_blocks = c.reshape(B, nb, ATTN_BLOCK, H).transpose(1, 0, 3, 2)

    def one_block(args):
        qb, cb, bi = args
        qpos = bi * ATTN_BLOCK + jnp.arange(ATTN_BLOCK)
        s = jnp.einsum('bqhd,bkhd->bhqk', qb, kf)
        s = s + cb[..., :, None] - c_keys[:, :, None, :]
        mask = (kpos[None, :] <= qpos[:, None]) & valid[None, :]
        s = jnp.where(mask, s, NEG_INF)
        p = jax.nn.softmax(s, axis=-1)
        return jnp.einsum('bhqk,bkhd->bqhd', p, vf)

    o = lax.map(one_block, (q_blocks, c_blocks, jnp.arange(nb)))
    return o.transpose(1, 0, 2, 3, 4).reshape(B, L, H * dh)


def gated_delta_rule(q, k, v, g, beta):
    B, H, L, dk = q.shape
    dv = v.shape[-1]
    C = DN_CHUNK
    N = L // C
    q = q * (dk ** -0.5)
    q, k, v = (t.reshape(B, H, N, C, t.shape[-1]) for t in (q, k, v))
    g, beta = (t.reshape(B, H, N, C) for t in (g, beta))
    gc = jnp.cumsum(g, axis=-1)
    tril = jnp.tril(jnp.ones((C, C), dtype=bool))
    strict = jnp.tril(jnp.ones((C, C), dtype=bool), -1)
    decay = jnp.exp(jnp.where(tril, gc[..., :, None] - gc[..., None, :], -jnp.inf))
    kb = k * beta[..., None]
    low = jnp.where(strict, jnp.einsum('bhncd,bhnsd->bhncs', kb, k) * decay, 0.0)
    a = low + jnp.eye(C, dtype=jnp.float32)
    rhs = jnp.concatenate([v * beta[..., None], kb * jnp.exp(gc)[..., None]], axis=-1)
    sol = lax.linalg.triangular_solve(a, rhs, left_side=True, lower=True, unit_diagonal=True)
    u, w = sol[..., :dv], sol[..., dv:]
    attn = jnp.einsum('bhncd,bhnsd->bhncs', q, k) * decay
    q_dec = q * jnp.exp(gc)[..., None]
    k_dec = k * jnp.exp(gc[..., -1:] - gc)[..., None]
    chunk_decay = jnp.exp(gc[..., -1])
    xs = tuple(jnp.moveaxis(t, 2, 0) for t in (q_dec, k_dec, u, w, attn, chunk_decay))

    def step(S, inp):
        qd, kd, uc, wc, ac, cd = inp
        v_new = uc - jnp.einsum('bhcd,bhde->bhce', wc, S)
        o = jnp.einsum('bhcd,bhde->bhce', qd, S) + jnp.einsum('bhcs,bhse->bhce', ac, v_new)
        S = S * cd[..., None, None] + jnp.einsum('bhcd,bhce->bhde', kd, v_new)
        return S, o

    S0 = jnp.zeros((B, H, dk, dv), jnp.float32)
    _, o = lax.scan(step, S0, xs)
    return jnp.moveaxis(o, 0, 2).reshape(B, H, L, dv)


def hybrid_mixer(h, valid, w_in, fox_forget_bias, dn_conv_w, dn_a_log, dn_dt_bias,
                 dn_out_norm_w, w_branch_fox, w_branch_dn, w_out):
    B, L, _ = h.shape
    proj = h @ w_in
    (fq, fk, fv, f_logit, dq, dk_, dv_, b_logit, a_logit, dz, ga, gb) = split_columns(proj, IN_SPLITS)
    vm = valid[None, :, None].astype(h.dtype)

    log_f = jax.nn.log_sigmoid(f_logit.astype(jnp.float32) + fox_forget_bias.astype(jnp.float32))
    rs_f = lambda t: t.reshape(B, L, FOX_HEADS, FOX_HEAD_DIM)
    o_fox = fox_attention(rs_f(fq), rs_f(fk), rs_f(fv), log_f, valid).astype(h.dtype)

    qkv = jax.nn.silu(causal_short_conv(jnp.concatenate([dq, dk_, dv_], axis=-1) * vm, dn_conv_w))
    qkv = qkv.astype(jnp.float32)
    rs_d = lambda t: t.reshape(B, L, DN_HEADS, DN_HEAD_DIM).transpose(0, 2, 1, 3)
    q_d = l2norm(rs_d(qkv[..., :DN_WIDTH]))
    k_d = l2norm(rs_d(qkv[..., DN_WIDTH:2 * DN_WIDTH]))
    v_d = rs_d(qkv[..., 2 * DN_WIDTH:])
    vmf = valid[None, None, :].astype(jnp.float32)
    beta = jax.nn.sigmoid(b_logit.astype(jnp.float32)).transpose(0, 2, 1) * vmf
    g = (-jnp.exp(dn_a_log.astype(jnp.float32))[None, :, None]
         * jax.nn.softplus(a_logit.astype(jnp.float32) + dn_dt_bias.astype(jnp.float32)).transpose(0, 2, 1)) * vmf
    o_dn = gated_delta_rule(q_d, k_d, v_d, g, beta).transpose(0, 2, 1, 3)
    z = dz.reshape(B, L, DN_HEADS, DN_HEAD_DIM)
    o_dn = (rmsnorm(o_dn, dn_out_norm_w) * jax.nn.silu(z.astype(jnp.float32))).reshape(B, L, DN_WIDTH)
    o_dn = o_dn.astype(h.dtype)

    y = jax.nn.sigmoid(ga) * (o_fox @ w_branch_fox) + jax.nn.sigmoid(gb) * (o_dn @ w_branch_dn)
    return y @ w_out


def swiglu(h, w_gate, w_up, w_down):
    return (jax.nn.silu(h @ w_gate) * (h @ w_up)) @ w_down


def setup_inputs(seed: int = 0) -> dict:
    key = jax.random.key(seed)
    ks = jax.random.split(key, 20)
    f32 = jnp.float32
    nrm = lambda k, shape, s: jax.random.normal(k, shape, f32) * s
    dt = jnp.exp(jax.random.uniform(ks[7], (DEPTH, DN_HEADS), f32)
                 * (np.log(0.1) - np.log(0.001)) + np.log(0.001))
    return {
        "x": nrm(ks[0], (BATCH, SEQ, D_MODEL), 1.0),
        "meta_tokens": nrm(ks[1], (N_META, D_MODEL), 1.0),
        "mix_norm_w": 1.0 + nrm(ks[2], (DEPTH, D_MODEL), 0.02),
        "w_in": nrm(ks[3], (DEPTH, D_MODEL, D_IN), D_MODEL ** -0.5),
        "fox_forget_bias": 3.0 + nrm(ks[4], (DEPTH, FOX_HEADS), 0.1),
        "dn_conv_w": nrm(ks[5], (DEPTH, CONV_WIDTH, 3 * DN_WIDTH), CONV_WIDTH ** -0.5),
        "dn_a_log": jnp.log(jax.random.uniform(ks[6], (DEPTH, DN_HEADS), f32, 1.0, 16.0)),
        "dn_dt_bias": dt + jnp.log(-jnp.expm1(-dt)),
        "dn_out_norm_w": 1.0 + nrm(ks[8], (DEPTH, DN_HEAD_DIM), 0.02),
        "w_branch_fox": nrm(ks[9], (DEPTH, FOX_WIDTH, D_MODEL), FOX_WIDTH ** -0.5),
        "w_branch_dn": nrm(ks[10], (DEPTH, DN_WIDTH, D_MODEL), DN_WIDTH ** -0.5),
        "w_out": nrm(ks[11], (DEPTH, D_MODEL, D_MODEL), D_MODEL ** -0.5),
        "ffn_norm_w": 1.0 + nrm(ks[12], (DEPTH, D_MODEL), 0.02),
        "w_ffn_gate": nrm(ks[13], (DEPTH, D_MODEL, D_FF), D_MODEL ** -0.5),
        "w_ffn_up": nrm(ks[14], (DEPTH, D_MODEL, D_FF), D_MODEL ** -0.5),
        "w_ffn_down": nrm(ks[15], (DEPTH, D_FF, D_MODEL), D_FF ** -0.5),
        "final_norm_w": 1.0 + nrm(ks[16], (D_MODEL,), 0.02),
    }


def reference(x, meta_tokens, mix_norm_w, w_in, fox_forget_bias, dn_conv_w, dn_a_log,
              dn_dt_bias, dn_out_norm_w, w_branch_fox, w_branch_dn, w_out, ffn_norm_w,
              w_ffn_gate, w_ffn_up, w_ffn_down, final_norm_w):
    B = x.shape[0]
    pad = jnp.zeros((B, META_PAD, D_MODEL), x.dtype)
    meta = jnp.broadcast_to(meta_tokens.astype(x.dtype)[None], (B, N_META, D_MODEL))
    h = jnp.concatenate([pad, meta, x], axis=1)
    L = h.shape[1]
    valid = jnp.arange(L) >= META_PAD
    for l in range(DEPTH):
        h = h + hybrid_mixer(rmsnorm(h, mix_norm_w[l]), valid, w_in[l], fox_forget_bias[l],
                             dn_conv_w[l], dn_a_log[l], dn_dt_bias[l], dn_out_norm_w[l],
                             w_branch_fox[l], w_branch_dn[l], w_out[l])
        h = h + swiglu(rmsnorm(h, ffn_norm_w[l]), w_ffn_gate[l], w_ffn_up[l], w_ffn_down[l])
    h = rmsnorm(h, final_norm_w)
    return h[:, PREFIX:, :]
```

```python
import math
from contextlib import ExitStack
import numpy as np
import concourse.bass as bass
import concourse.mybir as mybir
from concourse.bass_utils import run_bass_kernel_spmd

F32 = mybir.dt.float32
BF16 = mybir.dt.bfloat16
ALU = mybir.AluOpType
AF = mybir.ActivationFunctionType

ENGS = ("pe", "act", "dve", "pool", "sp")
NDMA = 64

NB = 33
NP = 16
D = 1024
KC = 8
DFF = 2816
NFT = 22
EPS = 1e-6
NEG = -30000.0
NO_POOL = True
BLOCK_BARRIER = False
PIPELINE1 = True
PIPELINE2 = True
SKIP_SAME_WAW = True
SDEPTH = 2
VND_DVE = True


class Buf:
    __slots__ = ("t", "w", "r", "name")

    def __init__(self, t, name=""):
        self.t = t
        self.w = None
        self.r = {}
        self.name = name

    def __getitem__(self, idx):
        return V(self, self.t[idx])

    def v(self, ap):
        return V(self, ap)


class V:
    __slots__ = ("buf", "ap")

    def __init__(self, buf, ap):
        self.buf = buf
        self.ap = ap


def Dr(ap):
    return V(None, ap)


class Prog:
    def __init__(self, nc):
        self.nc = nc
        self.q = {e: [] for e in ENGS}
        self.cnt = {e: 0 for e in ENGS}
        self.known = {e: {} for e in ENGS}
        self.ndma = 0
        self.ndma_q = [0, 0]
        self.dma_tok = [None] * NDMA
        self.esem = {}
        self.dsem = []

    def _need(self, eng, key, idx, waits):
        if key == ("e", "pe") and eng == "pe":
            return
        if self.known[eng].get(key, 0) >= idx:
            return
        if waits.get(key, 0) < idx:
            waits[key] = idx

    def emit(self, eng, fn, outs=(), ins=(), dma=False):
        if eng == "pool" and not dma and NO_POOL:
            eng = "dve"
        waits = {}
        for v in ins:
            if v is None or v.buf is None:
                continue
            if v.buf.w is not None:
                self._need(eng, v.buf.w[0], v.buf.w[1], waits)
        for v in outs:
            if v is None or v.buf is None:
                continue
            me = ("e", eng)
            if v.buf.w is not None and not (SKIP_SAME_WAW and v.buf.w[0] == me and not dma):
                self._need(eng, v.buf.w[0], v.buf.w[1], waits)
            for k_, i_ in v.buf.r.items():
                if SKIP_SAME_WAW and k_ == me and not dma:
                    continue
                self._need(eng, k_, i_, waits)
        if dma:
            half = NDMA // 2
            qi = 0 if eng == "sp" else 1
            n_ = self.ndma_q[qi]
            k = qi * half + n_ % half
            val = 16 * (n_ // half + 1)
            if self.dma_tok[k] is not None:
                self._need(eng, self.dma_tok[k][0], self.dma_tok[k][1], waits)
            tok = (("d", k), val)
            self.dma_tok[k] = tok
            self.ndma_q[qi] += 1
            self.ndma += 1
            inc = (("d", k), 16)
        else:
            self.cnt[eng] += 1
            tok = (("e", eng), self.cnt[eng])
            inc = (("e", eng), 1)
        for key, val_ in waits.items():
            self.known[eng][key] = val_
        self.q[eng].append((list(waits.items()), fn, inc))
        for v in ins:
            if v is None or v.buf is None:
                continue
            if v.buf.r.get(tok[0], 0) < tok[1]:
                v.buf.r[tok[0]] = tok[1]
        for v in outs:
            if v is None or v.buf is None:
                continue
            v.buf.w = tok
            v.buf.r = {}
        return tok

    def barrier(self):
        toks = [(("e", e), self.cnt[e]) for e in ENGS if self.cnt[e] > 0]
        toks += [t for t in self.dma_tok if t is not None]
        for e in ENGS:
            waits = {}
            for key, idx in toks:
                if key == ("e", e):
                    continue
                if self.known[e].get(key, 0) < idx:
                    waits[key] = idx
            for key, val_ in waits.items():
                self.known[e][key] = val_
            if waits:
                self.q[e].append((list(waits.items()), None, None))

    def mm(self, out, lhsT, rhs, start=True, stop=True):
        return self.emit("pe", lambda e: e.matmul(out.ap, lhsT.ap, rhs.ap, start=start, stop=stop),
                         outs=[out], ins=[lhsT, rhs])

    def tr(self, out, in_, ident):
        return self.emit("pe", lambda e: e.transpose(out.ap, in_.ap, ident.ap), outs=[out], ins=[in_, ident])

    def act(self, out, in_, func, bias=None, scale=None, accum=None):
        kw = {}
        ins = [in_]
        if bias is not None:
            if isinstance(bias, V):
                kw["bias"] = bias.ap
                ins.append(bias)
            else:
                kw["bias"] = bias
        if scale is not None:
            if isinstance(scale, V):
                kw["scale"] = scale.ap
                ins.append(scale)
            else:
                kw["scale"] = scale
        outs = [out]
        if accum is not None:
            kw["accum_out"] = accum.ap
            outs.append(accum)
        return self.emit("act", lambda e: e.activation(out.ap, in_.ap, func, **kw), outs=outs, ins=ins)

    def tt(self, eng, out, in0, in1, op):
        return self.emit(eng, lambda e: e.tensor_tensor(out.ap, in0.ap, in1.ap, op), outs=[out], ins=[in0, in1])

    def ts(self, eng, out, in0, s1, op0, s2=None, op1=None):
        ins = [in0]
        a1 = s1
        if isinstance(s1, V):
            a1 = s1.ap
            ins.append(s1)
        a2 = s2
        if isinstance(s2, V):
            a2 = s2.ap
            ins.append(s2)
        kw = {}
        if op1 is not None:
            kw["op1"] = op1
        return self.emit(eng, lambda e: e.tensor_scalar(out.ap, in0.ap, a1, a2, op0, **kw), outs=[out], ins=ins)

    def stt(self, eng, out, in0, s, in1, op0, op1):
        ins = [in0, in1]
        a = s
        if isinstance(s, V):
            a = s.ap
            ins.append(s)
        return self.emit(eng, lambda e: e.scalar_tensor_tensor(out.ap, in0.ap, a, in1.ap, op0, op1),
                         outs=[out], ins=ins)

    def copy(self, eng, out, in_):
        if eng == "act":
            return self.emit("act", lambda e: e.activation(out.ap, in_.ap, AF.Copy), outs=[out], ins=[in_])
        return self.emit(eng, lambda e: e.tensor_copy(out.ap, in_.ap), outs=[out], ins=[in_])

    def recip(self, out, in_):
        return self.emit("dve", lambda e: e.reciprocal(out.ap, in_.ap), outs=[out], ins=[in_])

    def memset(self, eng, out, val):
        return self.emit(eng, lambda e: e.memset(out.ap, val), outs=[out], ins=[])

    def dma(self, eng, out, in_):
        return self.emit(eng, lambda e: e.dma_start(out=out.ap, in_=in_.ap), outs=[out], ins=[in_], dma=True)

    def run(self, block):
        engmap = {"pe": "tensor", "act": "scalar", "dve": "vector", "pool": "gpsimd", "sp": "sync"}

        def sem_of(key):
            if key[0] == "e":
                return self.esem[key[1]]
            return self.dsem[key[1]]

        waits = {}
        for t in self.dma_tok:
            if t is not None and self.known["sp"].get(t[0], 0) < t[1]:
                waits[t[0]] = t[1]
        if waits:
            self.q["sp"].append((list(waits.items()), None, None))

        def make(ename):
            lst = self.q[ename]

            def body(eng):
                for waits_, fn, inc in lst:
                    for key, val in waits_:
                        eng.wait_ge(sem_of(key), val)
                    if fn is None:
                        continue
                    ins = fn(eng)
                    ins.then_inc(sem_of(inc[0]), inc[1])
            return body

        for ename in ENGS:
            getattr(block, engmap[ename])(make(ename))


class Arena:
    def __init__(self, tile, words):
        self.t = tile
        self.W = words
        self.lo = 0
        self.hi = words

    def alloc(self, dims, dt, name, top=False):
        n = 1
        for d_ in dims:
            n *= d_
        words = (n * (2 if dt == BF16 else 4) + 3) // 4
        if top:
            self.hi -= words
            off = self.hi
        else:
            off = self.lo
            self.lo += words
        assert self.lo <= self.hi, ("arena overflow", name, self.lo, self.hi)
        a = self.t[:, off:off + words]
        if dt == BF16:
            a = a.bitcast(BF16)
        if len(dims) == 2:
            a = a.rearrange("p (a b) -> p a b", a=dims[0])
        elif len(dims) == 3:
            a = a.rearrange("p (a b c) -> p a b c", a=dims[0], b=dims[1])
        return Buf(a, name)


def build(upto=3, dbg=False, nb1=NB, stage=99, only=None, skip1=False, nb2=NB, stage2=99):
    nc = bass.Bass("TRN2", target_bir_lowering=False)

    def din(name, shape):
        return nc.dram_tensor(name, shape, F32, kind="ExternalInput").ap()

    xs = din("xs", [NB * 128, D])
    xo = din("xo", [NP * 128, D])
    w_in = din("w_in", [D, 5648])
    w_bf = din("w_bf", [512, D])
    w_bd = din("w_bd", [512, D])
    w_out = din("w_out", [D, D])
    w_g = din("w_g", [D, DFF])
    w_u = din("w_u", [D, DFF])
    w_d = din("w_d", [DFF, D])
    cst = din("cst", [128, 1280 + 16])
    par = din("par", [128, 96])
    mskq = din("mskq", [128, 256])
    fnw_d = din("fnw", [128, D])
    y = nc.dram_tensor("y", [NP * 128, D], F32, kind="ExternalOutput").ap()
    if dbg:
        dbg_o = nc.dram_tensor("dbg", [128, 16 * 512], F32, kind="ExternalOutput").ap()

    with ExitStack() as es:
        P = Prog(nc)
        for e in ENGS:
            P.esem[e] = es.enter_context(nc.semaphore("s_" + e))
        for k in range(NDMA):
            P.dsem.append(es.enter_context(nc.semaphore("d_%d" % k)))

        def sb(shape, dt, name):
            return Buf(es.enter_context(nc.sbuf_tensor(name, shape, dt)), name)

        AW = 48300
        arena_t = es.enter_context(nc.sbuf_tensor("arena", [128, AW], F32))
        AR = Arena(arena_t, AW)
        banks = [Buf(es.enter_context(nc.psum_tensor("bank%d" % i, [128, 512], F32)), "bank%d" % i) for i in range(8)]
        bstate = {"i": 0, "n": 8}

        pools = {}

        def psum(pool=None):
            if pool is not None and pool in pools:
                lst, st = pools[pool]
                b = banks[lst[st[0] % len(lst)]]
                st[0] += 1
                return b
            b = banks[bstate["i"] % bstate["n"]]
            bstate["i"] += 1
            return b

        def pf(b, n=512, h=None):
            a = b.t[:, 0:n]
            if h is not None:
                a = a.rearrange("p (h n) -> p h n", h=h)
            return V(b, a)

        def pb(b, n=1024, h=None):
            a = b.t[:, 0:512].bitcast(BF16)[:, 0:n]
            if h is not None:
                a = a.rearrange("p (h n) -> p h n", h=h)
            return V(b, a)

        block = es.enter_context(nc.Block())

        CST = sb([128, 1296], F32, "CST")
        PAR = sb([128, 96], F32, "PAR")
        FNW = sb([128, D], F32, "FNW")
        CB = sb([128, 1280], BF16, "CB")
        MSK4 = sb([128, 4, 128], F32, "MSK4")
        NOTI4 = sb([128, 4, 128], F32, "NOTI4")
        BD4 = sb([128, 4, 128], BF16, "BD4")
        L14 = sb([128, 4, 128], BF16, "L14")
        L24 = sb([128, 4, 128], BF16, "L24")
        P.dma("sp", CST[:], Dr(cst))
        P.dma("sp", PAR[:], Dr(par))
        P.dma("sp", FNW[:], Dr(fnw_d))
        P.dma("pool", CB[:, 256:512], Dr(mskq))
        I_f = CST[:, 0:128]
        TRI = CST[:, 128:256]
        E127 = CST[:, 256:384]
        ONES_f = CST[:, 384:512]
        c_eps = CST[:, 1280:1281]
        c_one = CST[:, 1281:1282]
        c_lnq = CST[:, 1282:1283]
        c_padm = CST[:, 1283:1284]
        P.copy("dve", CB[:, 0:128], I_f)
        P.copy("dve", CB[:, 128:256], ONES_f)
        I_b = CB[:, 0:128]
        ONES_b = CB[:, 128:256]
        P.copy("dve", CB[:, 512:640], TRI)
        TRI_b = CB[:, 512:640]
        for h in range(4):
            P.copy("dve", CB[:, 640 + h * 128:640 + (h + 1) * 128], CST[:, 512:640])
        MSK4_b = CB[:, 640:1152]
        P.copy("dve", CB[:, 1152:1280], E127)
        E127_b = CB[:, 1152:1280]

        def split3(dst3, src, tmpf):
            P.copy("dve", dst3[:, 0, :], src)
            P.copy("dve", tmpf[:, 0, :], dst3[:, 0, :])
            P.tt("dve", tmpf[:, 1, :], src, tmpf[:, 0, :], ALU.subtract)
            P.copy("dve", dst3[:, 1, :], tmpf[:, 1, :])
            P.copy("dve", tmpf[:, 0, :], dst3[:, 1, :])
            P.tt("dve", tmpf[:, 2, :], tmpf[:, 1, :], tmpf[:, 0, :], ALU.subtract)
            P.copy("dve", dst3[:, 2, :], tmpf[:, 2, :])
        for h in range(4):
            P.copy("dve", MSK4[:, h, :], CST[:, 512:640])
            P.copy("dve", NOTI4[:, h, :], CST[:, 640:768])
            P.copy("dve", BD4[:, h, :], CST[:, 768:896])
            P.copy("dve", L14[:, h, :], CST[:, 896:1024])
            P.copy("dve", L24[:, h, :], CST[:, 1024:1152])
        NW1 = PAR[:, 0:8]
        NW2 = PAR[:, 8:16]
        mA = PAR[:, 65:66]
        mB = PAR[:, 66:67]
        NCO = sb([128, 16], F32, "NCO")
        P.memset("dve", NCO[:, :], -1.0)
        P.act(NCO[:, 4:8], PAR[:, 83:87], AF.Exp)
        P.ts("dve", NCO[:, 4:8], NCO[:, 4:8], -1.0, ALU.mult)

        def load_w(dst_buf, src_ap, kcn, ncol_lo, ncol_hi):
            for kc in range(kcn):
                P.dma("pool", dst_buf[:, kc, :], Dr(src_ap[kc * 128:(kc + 1) * 128, ncol_lo:ncol_hi]))

        def softplus_chain(lg_psum, n, vec, sgn, negcoef, out, tmp):
            yv, ny, ab, ex, l1 = (tmp[:, i * 8:i * 8 + n] for i in range(5))
            P.tt("dve", yv, lg_psum, vec, ALU.add)
            P.tt("dve", yv, yv, sgn, ALU.mult)
            P.ts("dve", ny, yv, -1.0, ALU.mult)
            P.tt("dve", ab, yv, ny, ALU.max)
            P.act(ex, ab, AF.Exp, scale=-1.0)
            P.act(l1, ex, AF.Ln, bias=c_one)
            P.ts("dve", ny, yv, 0.0, ALU.max)
            P.tt("dve", l1, l1, ny, ALU.add)
            P.tt("dve", out, l1, negcoef, ALU.mult)

        def norm_transpose(src_dram_rows, XT, XN, SCR, ST, HN_out, nw, evac_i, pool=None, part=None, HN_all=None):
            if part in (None, 0):
                P.dma("sp", XT[:, :], Dr(src_dram_rows))
                P.act(SCR[:, :], XT[:, :], AF.Square, accum=ST[:, 0:1])
                P.act(ST[:, 1:2], ST[:, 0:1], AF.Ln, scale=1.0 / D, bias=c_eps)
                P.act(ST[:, 2:3], ST[:, 1:2], AF.Exp, scale=-0.5)
                P.ts("dve", XN[:, :], XT[:, :], ST[:, 2:3], ALU.mult)
            if part == 0:
                return
            bk = psum(pool)
            for kc in range(KC):
                P.tr(V(bk, pb(bk).ap[:, kc * 128:(kc + 1) * 128]), XN[:, kc * 128:(kc + 1) * 128], I_b)
            if HN_all is not None:
                nwb = V(nw.buf, nw.ap.unsqueeze(2).to_broadcast([128, KC, 128]))
                P.tt("dve", HN_all, V(bk, pb(bk).ap.rearrange("p (k n) -> p k n", k=KC)), nwb, ALU.mult)
                return
            for kc in range(KC):
                src = V(bk, pb(bk).ap[:, kc * 128:(kc + 1) * 128])
                if True:
                    P.ts("dve", HN_out(kc), src, V(nw.buf, nw.ap[:, kc:kc + 1]), ALU.mult)
                else:
                    P.act(HN_out(kc), src, AF.Copy, scale=V(nw.buf, nw.ap[:, kc:kc + 1]))

        def dscr(name, shape):
            return nc.dram_tensor(name, shape, BF16, kind="Internal").ap()
        S_GAB = [Buf(dscr("s_gab%d" % i, [128, 8, 256]), "s_gab") for i in range(8)]
        S_BFD = [Buf(dscr("s_bfd%d" % i, [128, 4, 256]), "s_bfd") for i in range(8)]
        S_GU = [Buf(dscr("s_gu%d" % i, [128, 8, 256]), "s_gu") for i in range(NFT)]
        S_WO = Buf(dscr("s_wo", [128, 8, D]), "s_wo")
        S_WFX = Buf(dscr("s_wfx", [128, 8, 1024]), "s_wfx")
        S_WFV = Buf(dscr("s_wfv", [128, 8, 512]), "s_wfv")
        S_WLF = Buf(dscr("s_wlf", [128, 8, 8]), "s_wlf")
        S_WD = Buf(dscr("s_wd", [128, NFT, D]), "s_wd")

        def background_casts():
            r3 = lambda ap, lo, hi: ap[:, lo:hi].rearrange("(kc p) n -> p kc n", p=128)
            for kc in range(KC):
                P.dma("pool", S_WFX[:, kc, 0:512], Dr(w_in[kc * 128:(kc + 1) * 128, 512:1024]))
                P.dma("pool", S_WFX[:, kc, 512:1024], Dr(w_in[kc * 128:(kc + 1) * 128, 0:512]))
                P.dma("pool", S_WFV[:, kc, :], Dr(w_in[kc * 128:(kc + 1) * 128, 1024:1536]))
                P.dma("pool", S_WLF[:, kc, :], Dr(w_in[kc * 128:(kc + 1) * 128, 1536:1544]))
            for nt in range(8):
                P.dma("pool", S_GAB[nt][:, :, 0:128], Dr(r3(w_in, 3600 + nt * 128, 3600 + (nt + 1) * 128)))
                P.dma("pool", S_GAB[nt][:, :, 128:256], Dr(r3(w_in, 4624 + nt * 128, 4624 + (nt + 1) * 128)))
                P.dma("pool", S_BFD[nt][:, :, 0:128], Dr(r3(w_bf, nt * 128, (nt + 1) * 128)))
                P.dma("pool", S_BFD[nt][:, :, 128:256], Dr(r3(w_bd, nt * 128, (nt + 1) * 128)))
            for kc in range(KC):
                P.dma("pool", S_WO[:, kc, :], Dr(w_out[kc * 128:(kc + 1) * 128, :]))
            for ft in range(NFT):
                P.dma("pool", S_GU[ft][:, :, 0:128], Dr(r3(w_g, ft * 128, (ft + 1) * 128)))
                P.dma("pool", S_GU[ft][:, :, 128:256], Dr(r3(w_u, ft * 128, (ft + 1) * 128)))
            for kc in range(NFT):
                P.dma("pool", S_WD[:, kc, :], Dr(w_d[kc * 128:(kc + 1) * 128, :]))

        MIX = AR.alloc([NP, D], BF16, "MIX", top=True)
        mix_mark = AR.hi
        ODN = [AR.alloc([4, 128], BF16, "ODN%d" % p, top=True) for p in range(NP)]
        OFX = [AR.alloc([4, 128], BF16, "OFX%d" % p, top=True) for p in range(NP)]
        top_mark = AR.hi

        if upto >= 1 and not skip1:
            WDN = AR.alloc([KC, 2048], BF16, "WDN")
            WLG = AR.alloc([KC, 8], BF16, "WLG")
            r3w = lambda lo, hi: w_in[:, lo:hi].rearrange("(kc p) n -> p kc n", p=128)
            P.dma("pool", WDN[:, :, 0:1536], Dr(r3w(1544, 3080)))
            P.dma("pool", WLG[:, :, :], Dr(r3w(3080, 3088)))
            P.dma("pool", WDN[:, :, 1536:2048], Dr(r3w(3088, 3600)))
            if upto >= 3:
                background_casts()
            XT = [AR.alloc([D], F32, "XT%d" % i) for i in range(2)]
            XN = AR.alloc([D], BF16, "XN")
            SCR = None
            ST = [AR.alloc([4], F32, "ST%d" % i) for i in range(2)]
            HN = [AR.alloc([KC, 128], BF16, "HN%d" % i) for i in range(2)]
            HOWN = AR.alloc([KC, 128], BF16, "HOWN")
            U = AR.alloc([12, 131], F32, "U")
            Y = AR.alloc([12, 128], F32, "Y")
            class _Alias:
                def __init__(self, buf, ap):
                    self.buf, self.ap = buf, ap

                def __getitem__(self, idx):
                    return V(self.buf, self.ap[idx])
            SCR = _Alias(Y, Y.t[:, 0:8, :].rearrange("p a n -> p (a n)"))
            SQ = AR.alloc([8, 128], BF16, "SQ")
            RN = AR.alloc([8, 128], F32, "RN")
            QT = AR.alloc([4, 128], BF16, "QT")
            KT_ = AR.alloc([4, 128], BF16, "KTd")
            VT = AR.alloc([4, 128], BF16, "VT")
            TG = AR.alloc([4, 128], BF16, "TG")
            TG2 = AR.alloc([4, 128], BF16, "TG2")
            GHb = AR.alloc([4], BF16, "GHb")
            GLb = AR.alloc([4], BF16, "GLb")
            GHf = AR.alloc([4], F32, "GHf")
            GLf = AR.alloc([4], F32, "GLf")
            GRf = AR.alloc([4], F32, "GRf")
            EGC = AR.alloc([4, 128], F32, "EGC")
            DT = AR.alloc([4, 128], F32, "DT")
            DST = AR.alloc([4, 128], BF16, "DST")
            KTOK = AR.alloc([4, 128], BF16, "KTOK")
            VTOK = AR.alloc([4, 128], BF16, "VTOK")
            ATT = AR.alloc([4, 128], BF16, "ATT")
            QD = AR.alloc([4, 128], BF16, "QD")
            WW = [AR.alloc([4, 3, 128], BF16, "WW%d" % i) for i in range(2)]
            QF = AR.alloc([4, 128], BF16, "QF")
            O1N = AR.alloc([4, 128], BF16, "O1N")
            O2N = AR.alloc([4, 128], BF16, "O2N")
            DINV = AR.alloc([4, 128], BF16, "DINV")
            Y1 = AR.alloc([4, 128], BF16, "Y1")
            D2T = AR.alloc([4, 128], BF16, "D2T")
            TT = AR.alloc([4, 128], BF16, "TT")
            SF = AR.alloc([4, 128], F32, "SF")
            SB = AR.alloc([4, 128], BF16, "SB")
            XB = AR.alloc([4, 128], BF16, "XB")
            VN = AR.alloc([4, 128], BF16, "VN")
            VND = AR.alloc([4, 128], BF16, "VND")
            OACC = AR.alloc([4, 128], F32, "OACC")
            OSQ = AR.alloc([4, 128], BF16, "OSQ")
            RSTD = AR.alloc([4, 128], F32, "RSTD")
            SZ = AR.alloc([4, 128], F32, "SZ")
            RR = AR.alloc([8], F32, "RR")
            TMP = AR.alloc([40], F32, "TMP")
            SM = AR.alloc([40], F32, "SM")
            GC, NGC, BETA, NBETA, EG, NEG_EG, BKD, T1 = (SM[:, i * 4:(i + 1) * 4] for i in range(8))
            CW = PAR[:, 16:64]
            CTMP = AR.alloc([128], F32, "CTMP")
            HTMP = AR.alloc([KC, 128], BF16, "HTMP")

            P.memset("dve", U[:, :, :], 0.0)
            P.memset("dve", SF[:, :, :], 0.0)
            P.memset("dve", SB[:, :, :], 0.0)

            KT_2 = [KT_, AR.alloc([4, 128], BF16, "KTd1")]
            VTOK2 = [VTOK, AR.alloc([4, 128], BF16, "VTOK1")]
            KTOK2 = [KTOK, AR.alloc([4, 128], BF16, "KTOK1")]
            ATT2 = [ATT, AR.alloc([4, 128], BF16, "ATT1")]
            QD2 = [QD, AR.alloc([4, 128], BF16, "QD1")]
            EGC2 = [EGC, AR.alloc([4, 128], F32, "EGC1")]
            SM2 = [SM, AR.alloc([40], F32, "SM1")]
            O1N2 = [O1N, AR.alloc([4, 128], BF16, "O1N1")]
            O2N2 = [O2N, AR.alloc([4, 128], BF16, "O2N1")]
            WI2 = [AR.alloc([4, 3, 128], BF16, "WI%d" % i) for i in range(2)]

            def front(blk):
                par_ = blk % 2
                hn = HN[par_]
                KT_, VTOK, KTOK, ATT, QD, EGC, SM = KT_2[par_], VTOK2[par_], KTOK2[par_], ATT2[par_], QD2[par_], EGC2[par_], SM2[par_]
                O1N, O2N, W0 = O1N2[par_], O2N2[par_], WI2[par_]
                GC, NGC, BETA, NBETA, EG, NEG_EG, BKD, T1 = (SM[:, i * 4:(i + 1) * 4] for i in range(8))
                norm_transpose(xs[blk * 128:(blk + 1) * 128, :], XT[par_], XN, SCR, ST[par_],
                               lambda kc: hn[:, kc, :], NW1, blk, HN_all=hn[:, :, :])
                yield
                if blk > 0:
                    P.copy("pool", U[:, :, 0:3], U[:, :, 128:131])
                for g4 in range(3):
                    bk = psum()
                    for ti in range(4):
                        tile_ = g4 * 4 + ti
                        for kc in range(KC):
                            P.mm(V(bk, bk.t[:, ti * 128:(ti + 1) * 128]),
                                 WDN[:, kc, tile_ * 128:(tile_ + 1) * 128], hn[:, kc, :],
                                 start=(kc == 0), stop=(kc == KC - 1))
                    P.copy("act", U[:, g4 * 4:(g4 + 1) * 4, 3:131], pf(bk, 512, 4))
                    yield
                bl = psum()
                for kc in range(KC):
                    P.mm(V(bl, bl.t[:, 0:8]), hn[:, kc, :], WLG[:, kc, :], start=(kc == 0), stop=(kc == KC - 1))
                softplus_chain(V(bl, bl.t[:, 0:8]), 8, PAR[:, 67:75], PAR[:, 75:83], NCO[:, 0:8], RR[:, 0:8], TMP)
                yield
                cw3 = CW.ap.rearrange("p (t i) -> p t i", i=4)
                T4 = RN[:, 0:4, :]
                for g4 in range(3):
                    Yg = Y[:, g4 * 4:(g4 + 1) * 4, :]
                    wb_ = lambda i: V(PAR, cw3[:, g4 * 4:(g4 + 1) * 4, i:i + 1].to_broadcast([128, 4, 128]))
                    P.tt("dve", Yg, U[:, g4 * 4:(g4 + 1) * 4, 0:128], wb_(0), ALU.mult)
                    for i in range(1, 4):
                        P.tt("dve", T4, U[:, g4 * 4:(g4 + 1) * 4, i:i + 128], wb_(i), ALU.mult)
                        P.tt("dve", Yg, Yg, T4, ALU.add)
                    yield
                P.act(Y[:, :, :], Y[:, :, :], AF.Silu)
                P.act(SQ[:, :, :], Y[:, 0:8, :], AF.Square)
                for hf in range(2):
                    bk = psum()
                    P.mm(pf(bk), ONES_b, V(SQ, SQ.t[:, hf * 4:(hf + 1) * 4, :].rearrange("p a n -> p (a n)")))
                    P.act(RN[:, hf * 4:(hf + 1) * 4, :], pf(bk, 512, 4), AF.Ln, bias=c_eps)
                    P.act(RN[:, hf * 4:(hf + 1) * 4, :], RN[:, hf * 4:(hf + 1) * 4, :], AF.Exp, scale=-0.5,
                          bias=(c_lnq if hf == 0 else None))
                yield
                P.tt("dve", QT[:, :, :], Y[:, 0:4, :], RN[:, 0:4, :], ALU.mult)
                P.tt("dve", KT_[:, :, :], Y[:, 4:8, :], RN[:, 4:8, :], ALU.mult)
                P.copy("pool", VT[:, :, :], Y[:, 8:12, :])
                yield
                P.copy("dve", GHb[:, :], RR[:, 4:8])
                P.copy("dve", GHf[:, :], GHb[:, :])
                P.tt("dve", GRf[:, :], RR[:, 4:8], GHf[:, :], ALU.subtract)
                P.copy("dve", GLb[:, :], GRf[:, :])
                P.copy("dve", GLf[:, :], GLb[:, :])
                bg = psum()
                P.mm(V(bg, bg.t[:, 0:4]), TRI_b, GHb[:, :], start=True, stop=False)
                P.mm(V(bg, bg.t[:, 0:4]), TRI_b, GLb[:, :], start=False, stop=True)
                P.copy("dve", GC, V(bg, bg.t[:, 0:4]))
                P.ts("dve", NGC, GC, -1.0, ALU.mult)
                yield
                trib = V(CST, TRI.ap.unsqueeze(1).to_broadcast([128, 4, 128]))
                P.tt("dve", TG[:, :, :], trib, V(GHf, GHf.t[:, :].unsqueeze(2).to_broadcast([128, 4, 128])), ALU.mult)
                P.tt("dve", TG2[:, :, :], trib, V(GLf, GLf.t[:, :].unsqueeze(2).to_broadcast([128, 4, 128])), ALU.mult)
                tgf = V(TG, TG.t.rearrange("p a n -> p (a n)"))
                tg2f = V(TG2, TG2.t.rearrange("p a n -> p (a n)"))
                bx = psum()
                P.mm(pf(bx), ONES_b, tgf, start=True, stop=False)
                P.mm(pf(bx), ONES_b, tg2f, start=False, stop=True)
                P.act(EGC[:, :, :], pf(bx, 512, 4), AF.Exp)
                by = psum()
                P.mm(pf(by), ONES_b, tgf, start=True, stop=False)
                P.mm(pf(by), ONES_b, tg2f, start=False, stop=False)
                P.mm(pf(by), I_b, MSK4_b, start=False, stop=True)
                P.tt("dve", T1, RR[:, 0:4], GC, ALU.subtract)
                bxl = V(bx, pf(bx, 512, 4).ap[:, :, 127])
                P.emit("dve", lambda e: e.tensor_tensor(T1.ap, T1.ap, bxl.ap, ALU.add), outs=[T1], ins=[T1, bxl, EGC[:, :, :]])
                for h in range(4):
                    P.act(DT[:, h, :], V(by, by.t[:, h * 128:(h + 1) * 128]), AF.Exp, bias=V(SM, NGC.ap[:, h:h + 1]))
                yield
                P.tt("pool", DST[:, :, :], DT[:, :, :], NOTI4[:, :, :], ALU.mult)
                P.act(BETA, RR[:, 0:4], AF.Exp)
                P.ts("dve", NBETA, BETA, -1.0, ALU.mult)
                P.act(EG, GC, AF.Exp)
                P.ts("dve", NEG_EG, EG, -1.0, ALU.mult)
                P.act(BKD, T1, AF.Exp)
                P.tt("dve", QD[:, :, :], QT[:, :, :], EGC[:, :, :], ALU.mult)
                yield
                bt = psum()
                for h in range(4):
                    P.tr(V(bt, pb(bt).ap[:, h * 128:(h + 1) * 128]), KT_[:, h, :], I_b)
                    P.tr(V(bt, pb(bt).ap[:, 512 + h * 128:512 + (h + 1) * 128]), VT[:, h, :], I_b)
                P.copy("dve", KTOK[:, :, :], V(bt, pb(bt).ap[:, 0:512].rearrange("p (h n) -> p h n", h=4)))
                P.copy("dve", V(VTOK, VTOK.t.rearrange("p h n -> p (h n)")), V(bt, pb(bt).ap[:, 512:1024]))
                yield
                bkk = psum()
                bqk = psum()
                for h in range(4):
                    P.mm(V(bkk, bkk.t[:, h * 128:(h + 1) * 128]), KT_[:, h, :], KT_[:, h, :])
                    P.mm(V(bqk, bqk.t[:, h * 128:(h + 1) * 128]), KT_[:, h, :], QT[:, h, :])
                P.tt("dve", ATT[:, :, :], pf(bqk, 512, 4), DT[:, :, :], ALU.mult)
                for h in range(4):
                    P.stt("dve", QF[:, h, :], V(bkk, bkk.t[:, h * 128:(h + 1) * 128]), V(SM, NBETA.ap[:, h:h + 1]),
                          DST[:, h, :], ALU.mult, ALU.mult)
                P.copy("dve", W0[:, :, 0, :], V(CB, I_b.ap.unsqueeze(1).to_broadcast([128, 4, 128])))
                P.tt("dve", W0[:, :, 1, :], QF[:, :, :], BD4[:, :, :], ALU.mult)
                yield
                bq = psum()
                for h in range(4):
                    P.tr(V(bq, pb(bq).ap[:, h * 128:(h + 1) * 128]), QF[:, h, :], I_b)
                bq4 = V(bq, pb(bq).ap[:, 0:512].rearrange("p (h n) -> p h n", h=4))
                P.tt("dve", W0[:, :, 2, :], bq4, BD4[:, :, :], ALU.mult)
                P.tt("dve", O1N[:, :, :], bq4, L14[:, :, :], ALU.mult)
                P.tt("dve", O2N[:, :, :], bq4, L24[:, :, :], ALU.mult)
                yield

            def chain(blk):
                par_ = blk % 2
                hn = HN[par_]
                KT_, VTOK, KTOK, ATT, QD, EGC, SM = KT_2[par_], VTOK2[par_], KTOK2[par_], ATT2[par_], QD2[par_], EGC2[par_], SM2[par_]
                O1N, O2N = O1N2[par_], O2N2[par_]
                GC, NGC, BETA, NBETA, EG, NEG_EG, BKD, T1 = (SM[:, i * 4:(i + 1) * 4] for i in range(8))
                NR = 5
                for r in range(NR):
                    Wc = WI2[par_] if r == 0 else WW[(r - 1) % 2]
                    Wn = WW[r % 2]
                    last = (r == NR - 1)
                    for h in range(4):
                        bk = psum()
                        if not last:
                            P.mm(V(bk, bk.t[:, 0:256]), Wc[:, h, 2, :], V(Wc, Wc.t[:, h, 0:2, :].rearrange("p a n -> p (a n)")))
                            P.mm(V(bk, bk.t[:, 256:384]), Wc[:, h, 1, :], Wc[:, h, 2, :])
                            P.tt("dve", Wn[:, h, 0, :], V(bk, bk.t[:, 0:128]), Wc[:, h, 0, :], ALU.add)
                            P.copy("act", Wn[:, h, 1:3, :], V(bk, bk.t[:, 128:384].rearrange("p (a n) -> p a n", a=2)))
                        else:
                            P.mm(V(bk, bk.t[:, 0:128]), Wc[:, h, 2, :], Wc[:, h, 0, :])
                            P.tt("dve", Wn[:, h, 0, :], V(bk, bk.t[:, 0:128]), Wc[:, h, 0, :], ALU.add)
                    yield
                WD_ = WW[(NR - 1) % 2]
                prev = [WD_[:, h, 0, :] for h in range(4)]
                prev4 = WD_[:, :, 0, :]
                for lvl, (ON, DNX, YY, OUT4) in enumerate(((O1N, DINV, Y1, D2T), (O2N, DINV, Y1, TT))):
                    btr = psum()
                    for h in range(4):
                        P.tr(V(btr, pb(btr).ap[:, h * 128:(h + 1) * 128]), prev[h], I_b)
                    P.copy("dve", DNX[:, :, :], V(btr, pb(btr).ap[:, 0:512].rearrange("p (h n) -> p h n", h=4)))
                    by_ = psum()
                    for h in range(4):
                        P.mm(V(by_, by_.t[:, h * 128:(h + 1) * 128]), ON[:, h, :], prev[h])
                    P.copy("act", YY[:, :, :], pf(by_, 512, 4))
                    yield
                    bp_ = psum()
                    for h in range(4):
                        P.mm(V(bp_, bp_.t[:, h * 128:(h + 1) * 128]), DNX[:, h, :], YY[:, h, :])
                    P.tt("dve", OUT4[:, :, :], pf(bp_, 512, 4), prev4, ALU.add)
                    prev = [OUT4[:, h, :] for h in range(4)]
                    prev4 = OUT4[:, :, :]
                    yield
                b1 = psum()
                for h in range(4):
                    P.mm(V(b1, b1.t[:, h * 128:(h + 1) * 128]), KT_[:, h, :], SB[:, h, :])
                bc4 = lambda v_: V(v_.buf, v_.ap.unsqueeze(2).to_broadcast([128, 4, 128]))
                P.tt("dve", RSTD[:, :, :], pf(b1, 512, 4), bc4(NEG_EG), ALU.mult)
                P.tt("dve", XB[:, :, :], RSTD[:, :, :], VTOK[:, :, :], ALU.add)
                yield
                b2 = psum()
                for h in range(4):
                    P.mm(V(b2, b2.t[:, h * 128:(h + 1) * 128]), TT[:, h, :], XB[:, h, :])
                P.tt("dve", VN[:, :, :], pf(b2, 512, 4), bc4(BETA), ALU.mult)
                P.tt("dve", VND[:, :, :], pf(b2, 512, 4), bc4(BKD), ALU.mult)
                yield
                if blk > 0:
                    b3 = psum()
                    for h in range(4):
                        o_ = V(b3, b3.t[:, h * 128:(h + 1) * 128])
                        P.mm(o_, SB[:, h, :], QD[:, h, :], start=True, stop=False)
                        P.mm(o_, VN[:, h, :], ATT[:, h, :], start=False, stop=True)
                    if blk % 2 == 1:
                        P.ts("dve", OACC[:, :, :], pf(b3, 512, 4), mA, ALU.mult)
                    else:
                        P.stt("dve", OACC[:, :, :], pf(b3, 512, 4), mB, OACC[:, :, :], ALU.mult, ALU.add)
                b4 = psum()
                for h in range(4):
                    P.mm(V(b4, b4.t[:, h * 128:(h + 1) * 128]), KTOK[:, h, :], VND[:, h, :])
                cdb = V(EGC, EGC.t[:, :, 127:128].to_broadcast([128, 4, 128]))
                P.tt("dve", SF[:, :, :], SF[:, :, :], cdb, ALU.mult)
                P.tt("dve", SF[:, :, :], SF[:, :, :], pf(b4, 512, 4), ALU.add)
                P.copy("pool", SB[:, :, :], SF[:, :, :])
                yield
                if blk > 0:
                    if blk % 2 == 1:
                        P.ts("pool", HOWN[:, :, :], hn[:, :, :], mA, ALU.mult)
                    else:
                        P.ts("pool", HTMP[:, :, :], hn[:, :, :], mB, ALU.mult)
                        P.tt("pool", HOWN[:, :, :], HOWN[:, :, :], HTMP[:, :, :], ALU.add)
                        p = blk // 2 - 1
                        bz = psum()
                        for ti in range(4):
                            for kc in range(KC):
                                P.mm(V(bz, bz.t[:, ti * 128:(ti + 1) * 128]),
                                     WDN[:, kc, 1536 + ti * 128:1536 + (ti + 1) * 128], HOWN[:, kc, :],
                                     start=(kc == 0), stop=(kc == KC - 1))
                        P.act(SZ[:, :, :], pf(bz, 512, 4), AF.Silu)
                        yield
                        P.act(OSQ[:, :, :], OACC[:, :, :], AF.Square)
                        b5 = psum()
                        P.mm(pf(b5), ONES_b, V(OSQ, OSQ.t.rearrange("p a n -> p (a n)")))
                        P.act(RSTD[:, :, :], pf(b5, 512, 4), AF.Ln, scale=1.0 / 128, bias=c_eps)
                        P.act(RSTD[:, :, :], RSTD[:, :, :], AF.Exp, scale=-0.5)
                        P.stt("dve", RSTD[:, :, :], OACC[:, :, :], PAR[:, 64:65], RSTD[:, :, :], ALU.mult, ALU.mult)
                        P.tt("dve", ODN[p][:, :, :], RSTD[:, :, :], SZ[:, :, :], ALU.mult)
                yield

            def interleave(gens):
                gens = list(gens)
                while gens:
                    for g in list(gens):
                        try:
                            next(g)
                        except StopIteration:
                            gens.remove(g)

            interleave([front(0)])
            for blk in range(nb1):
                gl = [chain(blk)]
                if blk + 1 < nb1:
                    gl.append(front(blk + 1))
                if PIPELINE1:
                    interleave(gl)
                else:
                    for g in gl:
                        interleave([g])

            if dbg and upto == 1:
                DB = Buf(RSTD.t.rearrange("p h n -> p (h n)"), "DB")
                P.dma("sp", Dr(dbg_o[:, 15 * 512:16 * 512]), V(SF, SF.t.rearrange("p h n -> p (h n)")))
                extras = [(10, QT, None), (11, KT_2[0], None), (12, VT, None), (13, ATT2[0], None), (14, TT, None)]
                for slot, bufx, sub in extras:
                    if stage < 6:
                        break
                    src = bufx.t.rearrange("p h n -> p (h n)") if sub is None else None
                    if sub is None:
                        P.copy("dve", DB[:, :], V(bufx, src))
                    else:
                        for h in range(4):
                            P.copy("dve", DB[:, h * 128:(h + 1) * 128], bufx[:, h, sub, :])
                    P.dma("sp", Dr(dbg_o[:, slot * 512:(slot + 1) * 512]), DB[:, :])
                for p in range(min(10, (nb1 - 1) // 2) if stage >= 6 else 0):
                    P.copy("dve", DB[:, :], V(ODN[p], ODN[p].t.rearrange("p h n -> p (h n)")))
                    P.dma("sp", Dr(dbg_o[:, p * 512:(p + 1) * 512]), DB[:, :])

        if upto >= 2:
            P.barrier()
            AR.lo = 0
            bstate["n"] = 6
            OBK = [banks[6], banks[7]]
            WFX = AR.alloc([KC, 1024], BF16, "WFX")
            WFV = AR.alloc([KC, 512], BF16, "WFV")
            WLF = AR.alloc([KC, 8], BF16, "WLF")
            if upto >= 3 and not skip1:
                P.dma("sp", WFX[:, :, :], S_WFX[:, :, :])
                P.dma("sp", WFV[:, :, :], S_WFV[:, :, :])
                P.dma("sp", WLF[:, :, :], S_WLF[:, :, :])
            else:
                for kc in range(KC):
                    P.dma("pool", WFX[:, kc, 0:512], Dr(w_in[kc * 128:(kc + 1) * 128, 512:1024]))
                    P.dma("pool", WFX[:, kc, 512:1024], Dr(w_in[kc * 128:(kc + 1) * 128, 0:512]))
                    P.dma("pool", WFV[:, kc, :], Dr(w_in[kc * 128:(kc + 1) * 128, 1024:1536]))
                    P.dma("pool", WLF[:, kc, :], Dr(w_in[kc * 128:(kc + 1) * 128, 1536:1544]))
            KT = [AR.alloc([4, 128], BF16, "KT%d" % b) for b in range(NB)]
            VA = [AR.alloc([8, 65], BF16, "VA%d" % b) for b in range(NB)]
            XT = [AR.alloc([D], F32, "XT%d" % i) for i in range(2)]
            XN = AR.alloc([D], BF16, "XN")
            SCR = AR.alloc([D], BF16, "SCR")
            ST = [AR.alloc([4], F32, "ST%d" % i) for i in range(2)]
            HN = [AR.alloc([KC, 128], BF16, "HN%d" % i) for i in range(2)]
            HOWN = AR.alloc([KC, 128], BF16, "HOWN")
            QTP = AR.alloc([4, 128], BF16, "QTP")
            HTMP = AR.alloc([KC, 128], BF16, "HTMP")
            NEGC = AR.alloc([8, NB], F32, "NEGC")
            CREF = AR.alloc([8, NB], F32, "CREF")
            CP = [AR.alloc([8], F32, "CP%d" % i) for i in range(2)]
            LF = AR.alloc([8], F32, "LF")
            CP3 = AR.alloc([3, 8], BF16, "CP3")
            LF3 = AR.alloc([3, 8], BF16, "LF3")
            SPT = AR.alloc([3, 8], F32, "SPT")
            TMP = AR.alloc([40], F32, "TMP")
            BT = [AR.alloc([NB], F32, "BT%d" % i) for i in range(2)]
            PT = [AR.alloc([512], BF16, "PT%d" % i) for i in range(4)]
            OT = AR.alloc([512], BF16, "OT")
            RS = AR.alloc([8], F32, "RS")
            MSKA = CB[:, 256:384]
            MSKB = CB[:, 384:512]
            P.memset("dve", CP[1][:, :], 0.0)
            for b in range(NB):
                P.memset("pool", VA[b][:, :, 64:65], 1.0)
            ptc = [0]

            QTP2 = [QTP, AR.alloc([4, 128], BF16, "QTP1")]
            OBH = [Buf(banks[6 + (h // 4) % 2].t[:, (h % 4) * 65:(h % 4 + 1) * 65], "OBH%d" % h) for h in range(8)]

            pools["S"] = ([0, 1, 2, 3], [0])
            pools["J"] = ([4, 5], [0])

            def proj(blk):
                par_ = blk % 2
                hn = HN[par_]
                norm_transpose(xs[blk * 128:(blk + 1) * 128, :], XT[par_], XN, SCR, ST[par_],
                               lambda kc: hn[:, kc, :], NW1, blk, pool="J", part=0)
                yield
                norm_transpose(xs[blk * 128:(blk + 1) * 128, :], XT[par_], XN, SCR, ST[par_],
                               lambda kc: hn[:, kc, :], NW1, blk, pool="J", part=1, HN_all=hn[:, :, :])
                yield
                bk = psum("J")
                for ti in range(4):
                    for kc in range(KC):
                        P.mm(V(bk, bk.t[:, ti * 128:(ti + 1) * 128]), WFX[:, kc, ti * 128:(ti + 1) * 128], hn[:, kc, :],
                             start=(kc == 0), stop=(kc == KC - 1))
                P.copy("dve", KT[blk][:, :, :], pf(bk, 512, 4))
                yield
                bv = psum("J")
                for kc in range(KC):
                    P.mm(pf(bv), hn[:, kc, :], WFV[:, kc, :], start=(kc == 0), stop=(kc == KC - 1))
                P.copy("dve", VA[blk][:, :, 0:64], pf(bv, 512, 8))
                yield
                bl = psum("J")
                for kc in range(KC):
                    P.mm(V(bl, bl.t[:, 0:8]), hn[:, kc, :], WLF[:, kc, :], start=(kc == 0), stop=(kc == KC - 1))
                softplus_chain(V(bl, bl.t[:, 0:8]), 8, PAR[:, 87:95], NCO[:, 8:16], NCO[:, 8:16], LF[:, 0:8], TMP)
                yield
                cprev = CP[1 - par_]
                ccur = CP[par_]
                split3(CP3, cprev[:, :], SPT)
                split3(LF3, LF[:, :], SPT)
                bc = psum("J")
                for t in range(3):
                    P.mm(V(bc, bc.t[:, 0:8]), E127_b, CP3[:, t, :], start=(t == 0), stop=(t == 2))
                for t in range(3):
                    P.mm(V(bc, bc.t[:, 8:16]), TRI_b, LF3[:, t, :], start=(t == 0), stop=False)
                for t in range(3):
                    P.mm(V(bc, bc.t[:, 8:16]), E127_b, CP3[:, t, :], start=False, stop=(t == 2))
                P.copy("dve", CREF[:, :, blk], V(bc, bc.t[:, 0:8]))
                P.copy("dve", ccur[:, :], V(bc, bc.t[:, 8:16]))
                if blk == 0:
                    P.ts("dve", NEGC[:, :, blk], V(bc, bc.t[:, 8:16]), -1.0, ALU.mult, c_padm, ALU.add)
                else:
                    P.ts("dve", NEGC[:, :, blk], V(bc, bc.t[:, 8:16]), -1.0, ALU.mult)
                yield
                if blk == 0:
                    return
                if blk % 2 == 1:
                    P.ts("pool", HOWN[:, :, :], hn[:, :, :], mA, ALU.mult)
                    yield
                    return
                P.ts("pool", HTMP[:, :, :], hn[:, :, :], mB, ALU.mult)
                P.tt("pool", HOWN[:, :, :], HOWN[:, :, :], HTMP[:, :, :], ALU.add)
                p = blk // 2 - 1
                qtp = QTP2[p % 2]
                bqp = psum("J")
                for ti in range(4):
                    for kc in range(KC):
                        P.mm(V(bqp, bqp.t[:, ti * 128:(ti + 1) * 128]), WFX[:, kc, 512 + ti * 128:512 + (ti + 1) * 128],
                             HOWN[:, kc, :], start=(kc == 0), stop=(kc == KC - 1))
                P.copy("dve", qtp[:, :, :], pf(bqp, 512, 4))
                yield

            def attn(p):
                blk = 2 * p + 2
                blkA = blk - 1
                qtp = QTP2[p % 2]
                nk = blk + 1
                groups = [list(range(g, min(g + 4, nk))) for g in range(0, nk, 4)]
                for h in range(8):
                    hp, base = h // 2, (h % 2) * 64
                    ob = OBH[h]
                    bt_ = BT[h % 2]
                    P.ts("dve", bt_[:, 0:nk], NEGC[:, h, 0:nk], CREF[:, h, blkA:blkA + 1], ALU.add)

                    def s_group(kbs):
                        bs = psum("S")
                        for i, kb in enumerate(kbs):
                            o_ = V(bs, bs.t[:, i * 128:(i + 1) * 128])
                            masked = kb >= blkA
                            P.mm(o_, KT[kb][base:base + 64, hp, :], qtp[base:base + 64, hp, :], start=True, stop=not masked)
                            if masked:
                                P.mm(o_, I_b, MSKA if kb == blkA else MSKB, start=False, stop=True)
                        return bs

                    ng = len(groups)
                    bq_ = [s_group(groups[i]) for i in range(min(SDEPTH, ng))]
                    for gi, kbs in enumerate(groups):
                        bs = bq_.pop(0)
                        if gi + SDEPTH < ng:
                            bq_.append(s_group(groups[gi + SDEPTH]))
                        pt = PT[ptc[0] % 4]
                        ptc[0] += 1
                        for i, kb in enumerate(kbs):
                            P.act(pt[:, i * 128:(i + 1) * 128], V(bs, bs.t[:, i * 128:(i + 1) * 128]), AF.Exp,
                                  bias=bt_[:, kb:kb + 1], scale=0.125)
                        for i, kb in enumerate(kbs):
                            P.mm(ob[:, 0:65], pt[:, i * 128:(i + 1) * 128], VA[kb][:, h, :],
                                 start=(kb == 0), stop=(kb == nk - 1))
                        if gi % 2 == 1:
                            yield
                    P.recip(RS[:, h:h + 1], ob[:, 64:65])
                    P.ts("dve", OT[:, h * 64:(h + 1) * 64], ob[:, 0:64], RS[:, h:h + 1], ALU.mult)
                    yield
                bo = psum("J")
                for c4 in range(4):
                    P.tr(V(bo, pb(bo).ap[:, c4 * 128:(c4 + 1) * 128]), OT[:, c4 * 128:(c4 + 1) * 128], I_b)
                P.copy("dve", OFX[p][:, :, :], V(bo, pb(bo).ap[:, 0:512].rearrange("p (h n) -> p h n", h=4)))
                yield

            def seq(*gs):
                for g in gs:
                    yield from g

            def interleave2(gens):
                gens = list(gens)
                while gens:
                    for g in list(gens):
                        try:
                            next(g)
                        except StopIteration:
                            gens.remove(g)

            interleave2([seq(proj(0), proj(1), proj(2))])
            for p in range(NP):
                gl = [attn(p)]
                nxt = [proj(b) for b in (2 * p + 3, 2 * p + 4) if b < NB]
                if nxt:
                    gl.append(seq(*nxt))
                if PIPELINE2:
                    interleave2(gl)
                else:
                    for g in gl:
                        interleave2([g])
            bstate["n"] = 8

            if dbg and upto == 2:
                DB = AR.alloc([512], F32, "DB")
                for p in range(NP):
                    P.copy("dve", DB[:, :], V(OFX[p], OFX[p].t.rearrange("p h n -> p (h n)")))
                    P.dma("sp", Dr(dbg_o[:, p * 512:(p + 1) * 512]), DB[:, :])

        if upto >= 3:
            P.barrier()
            AR.lo = 0
            HNA = AR.alloc([KC, NP * 128], BF16, "HNA")
            YT = AR.alloc([KC, NP * 128], BF16, "YT")
            WO = AR.alloc([KC, D], BF16, "WO")
            WG2 = [AR.alloc([KC, 256], BF16, "WG2_%d" % i) for i in range(2)]
            WB2 = [AR.alloc([4, 256], BF16, "WB2_%d" % i) for i in range(2)]
            XT = [AR.alloc([D], F32, "XT%d" % i) for i in range(2)]
            XN = AR.alloc([D], BF16, "XN")
            SCR = AR.alloc([D], BF16, "SCR")
            ST = [AR.alloc([4], F32, "ST%d" % i) for i in range(2)]
            SGA = AR.alloc([512], F32, "SGA")
            SGB = AR.alloc([512], F32, "SGB")
            T1_ = AR.alloc([512], F32, "T1_")
            T2_ = AR.alloc([512], F32, "T2_")
            P.dma("sp", WO[:, :, :], S_WO[:, :, :])
            def hna_blocks(js):
                for j in js:
                    norm_transpose(xo[j * 128:(j + 1) * 128, :], XT[j % 2], XN, SCR, ST[j % 2],
                                   lambda kc: HNA[:, kc, j * 128:(j + 1) * 128], NW1, j, HN_all=HNA[:, :, j * 128:(j + 1) * 128])
            hna_blocks(range(0, 4))
            for nt in range(8):
                wg = WG2[nt % 2]
                wb = WB2[nt % 2]
                P.dma("sp", wg[:, :, :], S_GAB[nt][:, :, :])
                P.dma("sp", wb[:, :, :], S_BFD[nt][:, :, :])
                for tt_ in range(4):
                    if nt == 0 and tt_ < 3:
                        hna_blocks(range(4 * (tt_ + 1), 4 * (tt_ + 2)))
                    bga, bgb, byf, byd = psum(), psum(), psum(), psum()
                    for kc in range(KC):
                        P.mm(pf(bga), wg[:, kc, 0:128], HNA[:, kc, tt_ * 512:(tt_ + 1) * 512], start=(kc == 0), stop=(kc == KC - 1))
                    for kc in range(KC):
                        P.mm(pf(bgb), wg[:, kc, 128:256], HNA[:, kc, tt_ * 512:(tt_ + 1) * 512], start=(kc == 0), stop=(kc == KC - 1))
                    for q4 in range(4):
                        for kc in range(4):
                            pidx = tt_ * 4 + q4
                            P.mm(V(byf, byf.t[:, q4 * 128:(q4 + 1) * 128]), wb[:, kc, 0:128], OFX[pidx][:, kc, :],
                                 start=(kc == 0), stop=(kc == 3))
                    for q4 in range(4):
                        for kc in range(4):
                            pidx = tt_ * 4 + q4
                            P.mm(V(byd, byd.t[:, q4 * 128:(q4 + 1) * 128]), wb[:, kc, 128:256], ODN[pidx][:, kc, :],
                                 start=(kc == 0), stop=(kc == 3))
                    P.act(SGA[:, :], pf(bga), AF.Sigmoid)
                    P.act(SGB[:, :], pf(bgb), AF.Sigmoid)
                    P.tt("dve", T1_[:, :], SGA[:, :], pf(byf), ALU.mult)
                    P.tt("dve", T2_[:, :], SGB[:, :], pf(byd), ALU.mult)
                    P.tt("pool", YT[:, nt, tt_ * 512:(tt_ + 1) * 512], T1_[:, :], T2_[:, :], ALU.add)
            for j in range(NP):
                for mh in range(2):
                    bm = psum()
                    for kc in range(KC):
                        P.mm(pf(bm), YT[:, kc, j * 128:(j + 1) * 128], WO[:, kc, mh * 512:(mh + 1) * 512],
                             start=(kc == 0), stop=(kc == KC - 1))
                    P.copy("act" if mh else "dve", MIX[:, j, mh * 512:(mh + 1) * 512], pf(bm))

            if dbg and upto == 3:
                pass
            P.barrier()
            AR.lo = 0
            AR.hi = mix_mark
            ACTT = AR.alloc([NFT, 1024], BF16, "ACTT")
            WD = AR.alloc([NFT, D], BF16, "WD")
            H1N = AR.alloc([KC, 1024], BF16, "H1N")
            WGU = [AR.alloc([KC, 256], BF16, "WGU%d" % i) for i in range(3)]
            XT = [AR.alloc([D], F32, "XT%d" % i) for i in range(2)]
            H1 = AR.alloc([D], F32, "H1")
            H2 = AR.alloc([D], F32, "H2")
            OUT = [AR.alloc([D], F32, "OUT%d" % i) for i in range(2)]
            XN = AR.alloc([D], BF16, "XN")
            SCR = AR.alloc([D], BF16, "SCR")
            ST = [AR.alloc([4], F32, "ST%d" % i) for i in range(2)]
            SG = [AR.alloc([512], F32, "SG%d" % i) for i in range(2)]
            P.dma("sp", WD[:, 0:11, :], S_WD[:, 0:11, :])
            P.dma("sp", WD[:, 11:22, :], S_WD[:, 11:22, :])
            for t2 in range(2):
                for jl in range(8):
                    j = t2 * 8 + jl
                    xt = XT[j % 2]
                    st = ST[j % 2]
                    P.dma("sp", xt[:, :], Dr(xo[j * 128:(j + 1) * 128, :]))
                    P.tt("dve", H1[:, :], xt[:, :], MIX[:, j, :], ALU.add)
                    P.act(SCR[:, :], H1[:, :], AF.Square, accum=st[:, 0:1])
                    P.act(st[:, 1:2], st[:, 0:1], AF.Ln, scale=1.0 / D, bias=c_eps)
                    P.act(st[:, 2:3], st[:, 1:2], AF.Exp, scale=-0.5)
                    P.ts("dve", XN[:, :], H1[:, :], st[:, 2:3], ALU.mult)
                    bk = psum()
                    for kc in range(KC):
                        P.tr(V(bk, pb(bk).ap[:, kc * 128:(kc + 1) * 128]), XN[:, kc * 128:(kc + 1) * 128], I_b)
                    for kc in range(KC):
                        src = V(bk, pb(bk).ap[:, kc * 128:(kc + 1) * 128])
                        if True:
                            P.ts("dve", H1N[:, kc, jl * 128:(jl + 1) * 128], src, V(PAR, NW2.ap[:, kc:kc + 1]), ALU.mult)
                        else:
                            P.act(H1N[:, kc, jl * 128:(jl + 1) * 128], src, AF.Copy, scale=V(PAR, NW2.ap[:, kc:kc + 1]))
                for ft in range(NFT):
                    wgu = WGU[ft % 3]
                    P.dma("sp", wgu[:, :, :], S_GU[ft][:, :, :])
                    for hf in range(2):
                        bg_, bu_ = psum(), psum()
                        for kc in range(KC):
                            P.mm(pf(bg_), wgu[:, kc, 0:128], H1N[:, kc, hf * 512:(hf + 1) * 512], start=(kc == 0), stop=(kc == KC - 1))
                        for kc in range(KC):
                            P.mm(pf(bu_), wgu[:, kc, 128:256], H1N[:, kc, hf * 512:(hf + 1) * 512], start=(kc == 0), stop=(kc == KC - 1))
                        sg = SG[hf]
                        P.act(sg[:, :], pf(bg_), AF.Silu)
                        P.tt("dve", ACTT[:, ft, hf * 512:(hf + 1) * 512], sg[:, :], pf(bu_), ALU.mult)
                for jl in range(8):
                    j = t2 * 8 + jl
                    xt = XT[j % 2]
                    st = ST[j % 2]
                    out_ = OUT[j % 2]
                    P.dma("sp", xt[:, :], Dr(xo[j * 128:(j + 1) * 128, :]))
                    P.tt("pool", H1[:, :], xt[:, :], MIX[:, j, :], ALU.add)
                    for mh in range(2):
                        bd_ = psum()
                        for kc in range(NFT):
                            P.mm(pf(bd_), ACTT[:, kc, jl * 128:(jl + 1) * 128], WD[:, kc, mh * 512:(mh + 1) * 512],
                                 start=(kc == 0), stop=(kc == NFT - 1))
                        P.tt("dve", H2[:, mh * 512:(mh + 1) * 512], H1[:, mh * 512:(mh + 1) * 512], pf(bd_), ALU.add)
                    P.act(SCR[:, :], H2[:, :], AF.Square, accum=st[:, 0:1])
                    P.act(st[:, 1:2], st[:, 0:1], AF.Ln, scale=1.0 / D, bias=c_eps)
                    P.act(st[:, 2:3], st[:, 1:2], AF.Exp, scale=-0.5)
                    P.stt("dve", out_[:, :], H2[:, :], st[:, 2:3], FNW[:, :], ALU.mult, ALU.mult)
                    P.dma("sp", Dr(y[j * 128:(j + 1) * 128, :]), out_[:, :])

        P.run(block)
        print("instr counts", {e: len(P.q[e]) for e in ENGS}, "ndma", P.ndma, flush=True)
    return nc


_NC_CACHE = {}


def _consts():
    c = np.zeros((128, 1296), np.float32)
    idx = np.arange(128)
    c[:, 0:128] = np.eye(128)
    c[:, 128:256] = (idx[:, None] <= idx[None, :])
    c[127, 256:384] = 1.0
    c[:, 384:512] = 1.0
    c[:, 512:640] = np.where(idx[None, :] >= idx[:, None], 0.0, -1e9)
    c[:, 640:768] = 1.0 - np.eye(128)
    b32 = idx // 32
    c[:, 768:896] = (b32[:, None] == b32[None, :])
    c[:, 896:1024] = ((b32[:, None] % 2 == 1) & (b32[None, :] == b32[:, None] - 1))
    c[:, 1024:1152] = ((idx[:, None] >= 64) & (idx[None, :] < 64))
    c[:, 1280] = EPS
    c[:, 1281] = 1.0
    c[:, 1282] = -0.5 * math.log(128.0)
    c[:112, 1283] = NEG
    return c


def make_in_maps(x, meta_tokens, mix_norm_w, w_in, fox_forget_bias, dn_conv_w, dn_a_log, dn_dt_bias,
                 dn_out_norm_w, w_branch_fox, w_branch_dn, w_out, ffn_norm_w, w_ffn_gate, w_ffn_up,
                 w_ffn_down, final_norm_w):
    f = lambda a: np.ascontiguousarray(np.asarray(a, dtype=np.float32))
    x = f(x)
    cst = _consts()
    idx = np.arange(128)
    causal = np.where(idx[:, None] <= idx[None, :], 0.0, NEG).astype(np.float32)
    allm = np.full((128, 128), NEG, np.float32)
    zero = np.zeros((128, 128), np.float32)
    fnw = np.ascontiguousarray(np.broadcast_to(f(final_norm_w)[None, :], (128, D)))
    bc = lambda v: np.broadcast_to(f(v).reshape(1, -1), (128, f(v).size))
    in_maps = []
    for c in range(8):
        b, s = c // 2, c % 2
        xs = np.concatenate([np.zeros((112, D), np.float32), f(meta_tokens), x[b]], axis=0)
        own = [2 * p - 1 + s for p in range(1, NP + 1)]
        xo = np.concatenate([xs[j * 128:(j + 1) * 128] for j in own], axis=0)
        par = np.zeros((128, 96), np.float32)
        par[:, 0:8] = f(mix_norm_w)[0].reshape(8, 128).T
        par[:, 8:16] = f(ffn_norm_w)[0].reshape(8, 128).T
        par[:, 16:64] = f(dn_conv_w)[0].reshape(4, 12, 128).transpose(2, 1, 0).reshape(128, 48)
        par[:, 64] = f(dn_out_norm_w)[0]
        par[:, 65] = 1.0 if s == 0 else 0.0
        par[:, 66] = 0.0 if s == 0 else 1.0
        par[:, 71:75] = bc(dn_dt_bias[0])
        par[:, 75:79] = -1.0
        par[:, 79:83] = 1.0
        par[:, 83:87] = bc(dn_a_log[0])
        par[:, 87:95] = bc(fox_forget_bias[0])
        mskq = np.concatenate([causal, allm] if s == 0 else [zero, causal], axis=1)
        in_maps.append({
            "xs": np.ascontiguousarray(xs), "xo": np.ascontiguousarray(xo), "w_in": f(w_in)[0],
            "w_bf": f(w_branch_fox)[0], "w_bd": f(w_branch_dn)[0], "w_out": f(w_out)[0],
            "w_g": f(w_ffn_gate)[0], "w_u": f(w_ffn_up)[0], "w_d": f(w_ffn_down)[0],
            "cst": cst, "par": par, "mskq": np.ascontiguousarray(mskq), "fnw": fnw,
        })
    return in_maps


def kernel(**inputs):
    if "nc" not in _NC_CACHE:
        _NC_CACHE["nc"] = build()
    nc = _NC_CACHE["nc"]
    in_maps = make_in_maps(**inputs)
    res = run_bass_kernel_spmd(nc, in_maps, core_ids=list(range(8)))
    out = np.zeros((4, 4096, D), np.float32)
    for c in range(8):
        b, s = c // 2, c % 2
        yc = np.asarray(res.results[c]["y"])
        for p in range(1, NP + 1):
            blk = 2 * p - 1 + s
            out[b, (blk - 1) * 128:blk * 128, :] = yc[(p - 1) * 128:p * 128, :]
    return out
```

```python
import math
from contextlib import ExitStack
import numpy as np
import concourse.bass as bass
import concourse.mybir as mybir
from concourse.bass_utils import run_bass_kernel_spmd

F32 = mybir.dt.float32
BF16 = mybir.dt.bfloat16
ALU = mybir.AluOpType
AF = mybir.ActivationFunctionType

ENGS = ("pe", "act", "dve", "pool", "sp")
NDMA = 64

NB = 33
NP = 16
D = 1024
KC = 8
DFF = 2816
NFT = 22
EPS = 1e-6
NEG = -30000.0
NO_POOL = True
BLOCK_BARRIER = False
PIPELINE1 = True
PIPELINE2 = True
SKIP_SAME_WAW = True
SDEPTH = 2
VND_DVE = True


class Buf:
    __slots__ = ("t", "w", "r", "name")

    def __init__(self, t, name=""):
        self.t = t
        self.w = None
        self.r = {}
        self.name = name

    def __getitem__(self, idx):
        return V(self, self.t[idx])

    def v(self, ap):
        return V(self, ap)


class V:
    __slots__ = ("buf", "ap")

    def __init__(self, buf, ap):
        self.buf = buf
        self.ap = ap


def Dr(ap):
    return V(None, ap)


class Prog:
    def __init__(self, nc):
        self.nc = nc
        self.q = {e: [] for e in ENGS}
        self.cnt = {e: 0 for e in ENGS}
        self.known = {e: {} for e in ENGS}
        self.ndma = 0
        self.ndma_q = [0, 0]
        self.dma_tok = [None] * NDMA
        self.esem = {}
        self.dsem = []

    def _need(self, eng, key, idx, waits):
        if key == ("e", "pe") and eng == "pe":
            return
        if self.known[eng].get(key, 0) >= idx:
            return
        if waits.get(key, 0) < idx:
            waits[key] = idx

    def emit(self, eng, fn, outs=(), ins=(), dma=False):
        if eng == "pool" and not dma and NO_POOL:
            eng = "dve"
        waits = {}
        for v in ins:
            if v is None or v.buf is None:
                continue
            if v.buf.w is not None:
                self._need(eng, v.buf.w[0], v.buf.w[1], waits)
        for v in outs:
            if v is None or v.buf is None:
                continue
            me = ("e", eng)
            if v.buf.w is not None and not (SKIP_SAME_WAW and v.buf.w[0] == me and not dma):
                self._need(eng, v.buf.w[0], v.buf.w[1], waits)
            for k_, i_ in v.buf.r.items():
                if SKIP_SAME_WAW and k_ == me and not dma:
                    continue
                self._need(eng, k_, i_, waits)
        if dma:
            half = NDMA // 2
            qi = 0 if eng == "sp" else 1
            n_ = self.ndma_q[qi]
            k = qi * half + n_ % half
            val = 16 * (n_ // half + 1)
            if self.dma_tok[k] is not None:
                self._need(eng, self.dma_tok[k][0], self.dma_tok[k][1], waits)
            tok = (("d", k), val)
            self.dma_tok[k] = tok
            self.ndma_q[qi] += 1
            self.ndma += 1
            inc = (("d", k), 16)
        else:
            self.cnt[eng] += 1
            tok = (("e", eng), self.cnt[eng])
            inc = (("e", eng), 1)
        for key, val_ in waits.items():
            self.known[eng][key] = val_
        self.q[eng].append((list(waits.items()), fn, inc))
        for v in ins:
            if v is None or v.buf is None:
                continue
            if v.buf.r.get(tok[0], 0) < tok[1]:
                v.buf.r[tok[0]] = tok[1]
        for v in outs:
            if v is None or v.buf is None:
                continue
            v.buf.w = tok
            v.buf.r = {}
        return tok

    def barrier(self):
        toks = [(("e", e), self.cnt[e]) for e in ENGS if self.cnt[e] > 0]
        toks += [t for t in self.dma_tok if t is not None]
        for e in ENGS:
            waits = {}
            for key, idx in toks:
                if key == ("e", e):
                    continue
                if self.known[e].get(key, 0) < idx:
                    waits[key] = idx
            for key, val_ in waits.items():
                self.known[e][key] = val_
            if waits:
                self.q[e].append((list(waits.items()), None, None))

    def mm(self, out, lhsT, rhs, start=True, stop=True):
        return self.emit("pe", lambda e: e.matmul(out.ap, lhsT.ap, rhs.ap, start=start, stop=stop),
                         outs=[out], ins=[lhsT, rhs])

    def tr(self, out, in_, ident):
        return self.emit("pe", lambda e: e.transpose(out.ap, in_.ap, ident.ap), outs=[out], ins=[in_, ident])

    def act(self, out, in_, func, bias=None, scale=None, accum=None):
        kw = {}
        ins = [in_]
        if bias is not None:
            if isinstance(bias, V):
                kw["bias"] = bias.ap
                ins.append(bias)
            else:
                kw["bias"] = bias
        if scale is not None:
            if isinstance(scale, V):
                kw["scale"] = scale.ap
                ins.append(scale)
            else:
                kw["scale"] = scale
        outs = [out]
        if accum is not None:
            kw["accum_out"] = accum.ap
            outs.append(accum)
        return self.emit("act", lambda e: e.activation(out.ap, in_.ap, func, **kw), outs=outs, ins=ins)

    def tt(self, eng, out, in0, in1, op):
        return self.emit(eng, lambda e: e.tensor_tensor(out.ap, in0.ap, in1.ap, op), outs=[out], ins=[in0, in1])

    def ts(self, eng, out, in0, s1, op0, s2=None, op1=None):
        ins = [in0]
        a1 = s1
        if isinstance(s1, V):
            a1 = s1.ap
            ins.append(s1)
        a2 = s2
        if isinstance(s2, V):
            a2 = s2.ap
            ins.append(s2)
        kw = {}
        if op1 is not None:
            kw["op1"] = op1
        return self.emit(eng, lambda e: e.tensor_scalar(out.ap, in0.ap, a1, a2, op0, **kw), outs=[out], ins=ins)

    def stt(self, eng, out, in0, s, in1, op0, op1):
        ins = [in0, in1]
        a = s
        if isinstance(s, V):
            a = s.ap
            ins.append(s)
        return self.emit(eng, lambda e: e.scalar_tensor_tensor(out.ap, in0.ap, a, in1.ap, op0, op1),
                         outs=[out], ins=ins)

    def copy(self, eng, out, in_):
        if eng == "act":
            return self.emit("act", lambda e: e.activation(out.ap, in_.ap, AF.Copy), outs=[out], ins=[in_])
        return self.emit(eng, lambda e: e.tensor_copy(out.ap, in_.ap), outs=[out], ins=[in_])

    def recip(self, out, in_):
        return self.emit("dve", lambda e: e.reciprocal(out.ap, in_.ap), outs=[out], ins=[in_])

    def memset(self, eng, out, val):
        return self.emit(eng, lambda e: e.memset(out.ap, val), outs=[out], ins=[])

    def dma(self, eng, out, in_):
        return self.emit(eng, lambda e: e.dma_start(out=out.ap, in_=in_.ap), outs=[out], ins=[in_], dma=True)

    def run(self, block):
        engmap = {"pe": "tensor", "act": "scalar", "dve": "vector", "pool": "gpsimd", "sp": "sync"}

        def sem_of(key):
            if key[0] == "e":
                return self.esem[key[1]]
            return self.dsem[key[1]]

        waits = {}
        for t in self.dma_tok:
            if t is not None and self.known["sp"].get(t[0], 0) < t[1]:
                waits[t[0]] = t[1]
        if waits:
            self.q["sp"].append((list(waits.items()), None, None))

        def make(ename):
            lst = self.q[ename]

            def body(eng):
                for waits_, fn, inc in lst:
                    for key, val in waits_:
                        eng.wait_ge(sem_of(key), val)
                    if fn is None:
                        continue
                    ins = fn(eng)
                    ins.then_inc(sem_of(inc[0]), inc[1])
            return body

        for ename in ENGS:
            getattr(block, engmap[ename])(make(ename))


class Arena:
    def __init__(self, tile, words):
        self.t = tile
        self.W = words
        self.lo = 0
        self.hi = words

    def alloc(self, dims, dt, name, top=False):
        n = 1
        for d_ in dims:
            n *= d_
        words = (n * (2 if dt == BF16 else 4) + 3) // 4
        if top:
            self.hi -= words
            off = self.hi
        else:
            off = self.lo
            self.lo += words
        assert self.lo <= self.hi, ("arena overflow", name, self.lo, self.hi)
        a = self.t[:, off:off + words]
        if dt == BF16:
            a = a.bitcast(BF16)
        if len(dims) == 2:
            a = a.rearrange("p (a b) -> p a b", a=dims[0])
        elif len(dims) == 3:
            a = a.rearrange("p (a b c) -> p a b c", a=dims[0], b=dims[1])
        return Buf(a, name)


def build(upto=3, dbg=False, nb1=NB, stage=99, only=None, skip1=False, nb2=NB, stage2=99):
    nc = bass.Bass("TRN2", target_bir_lowering=False)

    def din(name, shape):
        return nc.dram_tensor(name, shape, F32, kind="ExternalInput").ap()

    xs = din("xs", [NB * 128, D])
    xo = din("xo", [NP * 128, D])
    w_in = din("w_in", [D, 5648])
    w_bf = din("w_bf", [512, D])
    w_bd = din("w_bd", [512, D])
    w_out = din("w_out", [D, D])
    w_g = din("w_g", [D, DFF])
    w_u = din("w_u", [D, DFF])
    w_d = din("w_d", [DFF, D])
    cst = din("cst", [128, 1280 + 16])
    par = din("par", [128, 96])
    mskq = din("mskq", [128, 256])
    fnw_d = din("fnw", [128, D])
    y = nc.dram_tensor("y", [NP * 128, D], F32, kind="ExternalOutput").ap()
    if dbg:
        dbg_o = nc.dram_tensor("dbg", [128, 16 * 512], F32, kind="ExternalOutput").ap()

    with ExitStack() as es:
        P = Prog(nc)
        for e in ENGS:
            P.esem[e] = es.enter_context(nc.semaphore("s_" + e))
        for k in range(NDMA):
            P.dsem.append(es.enter_context(nc.semaphore("d_%d" % k)))

        def sb(shape, dt, name):
            return Buf(es.enter_context(nc.sbuf_tensor(name, shape, dt)), name)

        AW = 48300
        arena_t = es.enter_context(nc.sbuf_tensor("arena", [128, AW], F32))
        AR = Arena(arena_t, AW)
        banks = [Buf(es.enter_context(nc.psum_tensor("bank%d" % i, [128, 512], F32)), "bank%d" % i) for i in range(8)]
        bstate = {"i": 0, "n": 8}

        pools = {}

        def psum(pool=None):
            if pool is not None and pool in pools:
                lst, st = pools[pool]
                b = banks[lst[st[0] % len(lst)]]
                st[0] += 1
                return b
            b = banks[bstate["i"] % bstate["n"]]
            bstate["i"] += 1
            return b

        def pf(b, n=512, h=None):
            a = b.t[:, 0:n]
            if h is not None:
                a = a.rearrange("p (h n) -> p h n", h=h)
            return V(b, a)

        def pb(b, n=1024, h=None):
            a = b.t[:, 0:512].bitcast(BF16)[:, 0:n]
            if h is not None:
                a = a.rearrange("p (h n) -> p h n", h=h)
            return V(b, a)

        block = es.enter_context(nc.Block())

        CST = sb([128, 1296], F32, "CST")
        PAR = sb([128, 96], F32, "PAR")
        FNW = sb([128, D], F32, "FNW")
        CB = sb([128, 1280], BF16, "CB")
        MSK4 = sb([128, 4, 128], F32, "MSK4")
        NOTI4 = sb([128, 4, 128], F32, "NOTI4")
        BD4 = sb([128, 4, 128], BF16, "BD4")
        L14 = sb([128, 4, 128], BF16, "L14")
        L24 = sb([128, 4, 128], BF16, "L24")
        P.dma("sp", CST[:], Dr(cst))
        P.dma("sp", PAR[:], Dr(par))
        P.dma("sp", FNW[:], Dr(fnw_d))
        P.dma("pool", CB[:, 256:512], Dr(mskq))
        I_f = CST[:, 0:128]
        TRI = CST[:, 128:256]
        E127 = CST[:, 256:384]
        ONES_f = CST[:, 384:512]
        c_eps = CST[:, 1280:1281]
        c_one = CST[:, 1281:1282]
        c_lnq = CST[:, 1282:1283]
        c_padm = CST[:, 1283:1284]
        P.copy("dve", CB[:, 0:128], I_f)
        P.copy("dve", CB[:, 128:256], ONES_f)
        I_b = CB[:, 0:128]
        ONES_b = CB[:, 128:256]
        P.copy("dve", CB[:, 512:640], TRI)
        TRI_b = CB[:, 512:640]
        for h in range(4):
            P.copy("dve", CB[:, 640 + h * 128:640 + (h + 1) * 128], CST[:, 512:640])
        MSK4_b = CB[:, 640:1152]
        P.copy("dve", CB[:, 1152:1280], E127)
        E127_b = CB[:, 1152:1280]

        def split3(dst3, src, tmpf):
            P.copy("dve", dst3[:, 0, :], src)
            P.copy("dve", tmpf[:, 0, :], dst3[:, 0, :])
            P.tt("dve", tmpf[:, 1, :], src, tmpf[:, 0, :], ALU.subtract)
            P.copy("dve", dst3[:, 1, :], tmpf[:, 1, :])
            P.copy("dve", tmpf[:, 0, :], dst3[:, 1, :])
            P.tt("dve", tmpf[:, 2, :], tmpf[:, 1, :], tmpf[:, 0, :], ALU.subtract)
            P.copy("dve", dst3[:, 2, :], tmpf[:, 2, :])
        for h in range(4):
            P.copy("dve", MSK4[:, h, :], CST[:, 512:640])
            P.copy("dve", NOTI4[:, h, :], CST[:, 640:768])
            P.copy("dve", BD4[:, h, :], CST[:, 768:896])
            P.copy("dve", L14[:, h, :], CST[:, 896:1024])
            P.copy("dve", L24[:, h, :], CST[:, 1024:1152])
        NW1 = PAR[:, 0:8]
        NW2 = PAR[:, 8:16]
        mA = PAR[:, 65:66]
        mB = PAR[:, 66:67]
        NCO = sb([128, 16], F32, "NCO")
        P.memset("dve", NCO[:, :], -1.0)
        P.act(NCO[:, 4:8], PAR[:, 83:87], AF.Exp)
        P.ts("dve", NCO[:, 4:8], NCO[:, 4:8], -1.0, ALU.mult)

        def load_w(dst_buf, src_ap, kcn, ncol_lo, ncol_hi):
            for kc in range(kcn):
                P.dma("pool", dst_buf[:, kc, :], Dr(src_ap[kc * 128:(kc + 1) * 128, ncol_lo:ncol_hi]))

        def softplus_chain(lg_psum, n, vec, sgn, negcoef, out, tmp):
            yv, ny, ab, ex, l1 = (tmp[:, i * 8:i * 8 + n] for i in range(5))
            P.tt("dve", yv, lg_psum, vec, ALU.add)
            P.tt("dve", yv, yv, sgn, ALU.mult)
            P.ts("dve", ny, yv, -1.0, ALU.mult)
            P.tt("dve", ab, yv, ny, ALU.max)
            P.act(ex, ab, AF.Exp, scale=-1.0)
            P.act(l1, ex, AF.Ln, bias=c_one)
            P.ts("dve", ny, yv, 0.0, ALU.max)
            P.tt("dve", l1, l1, ny, ALU.add)
            P.tt("dve", out, l1, negcoef, ALU.mult)

        def norm_transpose(src_dram_rows, XT, XN, SCR, ST, HN_out, nw, evac_i, pool=None, part=None, HN_all=None):
            if part in (None, 0):
                P.dma("sp", XT[:, :], Dr(src_dram_rows))
                P.act(SCR[:, :], XT[:, :], AF.Square, accum=ST[:, 0:1])
                P.act(ST[:, 1:2], ST[:, 0:1], AF.Ln, scale=1.0 / D, bias=c_eps)
                P.act(ST[:, 2:3], ST[:, 1:2], AF.Exp, scale=-0.5)
                P.ts("dve", XN[:, :], XT[:, :], ST[:, 2:3], ALU.mult)
            if part == 0:
                return
            bk = psum(pool)
            for kc in range(KC):
                P.tr(V(bk, pb(bk).ap[:, kc * 128:(kc + 1) * 128]), XN[:, kc * 128:(kc + 1) * 128], I_b)
            if HN_all is not None:
                nwb = V(nw.buf, nw.ap.unsqueeze(2).to_broadcast([128, KC, 128]))
                P.tt("dve", HN_all, V(bk, pb(bk).ap.rearrange("p (k n) -> p k n", k=KC)), nwb, ALU.mult)
                return
            for kc in range(KC):
                src = V(bk, pb(bk).ap[:, kc * 128:(kc + 1) * 128])
                if True:
                    P.ts("dve", HN_out(kc), src, V(nw.buf, nw.ap[:, kc:kc + 1]), ALU.mult)
                else:
                    P.act(HN_out(kc), src, AF.Copy, scale=V(nw.buf, nw.ap[:, kc:kc + 1]))

        def dscr(name, shape):
            return nc.dram_tensor(name, shape, BF16, kind="Internal").ap()
        S_GAB = [Buf(dscr("s_gab%d" % i, [128, 8, 256]), "s_gab") for i in range(8)]
        S_BFD = [Buf(dscr("s_bfd%d" % i, [128, 4, 256]), "s_bfd") for i in range(8)]
        S_GU = [Buf(dscr("s_gu%d" % i, [128, 8, 256]), "s_gu") for i in range(NFT)]
        S_WO = Buf(dscr("s_wo", [128, 8, D]), "s_wo")
        S_WFX = Buf(dscr("s_wfx", [128, 8, 1024]), "s_wfx")
        S_WFV = Buf(dscr("s_wfv", [128, 8, 512]), "s_wfv")
        S_WLF = Buf(dscr("s_wlf", [128, 8, 8]), "s_wlf")
        S_WD = Buf(dscr("s_wd", [128, NFT, D]), "s_wd")

        def background_casts():
            r3 = lambda ap, lo, hi: ap[:, lo:hi].rearrange("(kc p) n -> p kc n", p=128)
            for kc in range(KC):
                P.dma("pool", S_WFX[:, kc, 0:512], Dr(w_in[kc * 128:(kc + 1) * 128, 512:1024]))
                P.dma("pool", S_WFX[:, kc, 512:1024], Dr(w_in[kc * 128:(kc + 1) * 128, 0:512]))
                P.dma("pool", S_WFV[:, kc, :], Dr(w_in[kc * 128:(kc + 1) * 128, 1024:1536]))
                P.dma("pool", S_WLF[:, kc, :], Dr(w_in[kc * 128:(kc + 1) * 128, 1536:1544]))
            for nt in range(8):
                P.dma("pool", S_GAB[nt][:, :, 0:128], Dr(r3(w_in, 3600 + nt * 128, 3600 + (nt + 1) * 128)))
                P.dma("pool", S_GAB[nt][:, :, 128:256], Dr(r3(w_in, 4624 + nt * 128, 4624 + (nt + 1) * 128)))
                P.dma("pool", S_BFD[nt][:, :, 0:128], Dr(r3(w_bf, nt * 128, (nt + 1) * 128)))
                P.dma("pool", S_BFD[nt][:, :, 128:256], Dr(r3(w_bd, nt * 128, (nt + 1) * 128)))
            for kc in range(KC):
                P.dma("pool", S_WO[:, kc, :], Dr(w_out[kc * 128:(kc + 1) * 128, :]))
            for ft in range(NFT):
                P.dma("pool", S_GU[ft][:, :, 0:128], Dr(r3(w_g, ft * 128, (ft + 1) * 128)))
                P.dma("pool", S_GU[ft][:, :, 128:256], Dr(r3(w_u, ft * 128, (ft + 1) * 128)))
            for kc in range(NFT):
                P.dma("pool", S_WD[:, kc, :], Dr(w_d[kc * 128:(kc + 1) * 128, :]))

        MIX = AR.alloc([NP, D], BF16, "MIX", top=True)
        mix_mark = AR.hi
        ODN = [AR.alloc([4, 128], BF16, "ODN%d" % p, top=True) for p in range(NP)]
        OFX = [AR.alloc([4, 128], BF16, "OFX%d" % p, top=True) for p in range(NP)]
        top_mark = AR.hi

        if upto >= 1 and not skip1:
            WDN = AR.alloc([KC, 2048], BF16, "WDN")
            WLG = AR.alloc([KC, 8], BF16, "WLG")
            r3w = lambda lo, hi: w_in[:, lo:hi].rearrange("(kc p) n -> p kc n", p=128)
            P.dma("pool", WDN[:, :, 0:1536], Dr(r3w(1544, 3080)))
            P.dma("pool", WLG[:, :, :], Dr(r3w(3080, 3088)))
            P.dma("pool", WDN[:, :, 1536:2048], Dr(r3w(3088, 3600)))
            if upto >= 3:
                background_casts()
            XT = [AR.alloc([D], F32, "XT%d" % i) for i in range(2)]
            XN = AR.alloc([D], BF16, "XN")
            SCR = None
            ST = [AR.alloc([4], F32, "ST%d" % i) for i in range(2)]
            HN = [AR.alloc([KC, 128], BF16, "HN%d" % i) for i in range(2)]
            HOWN = AR.alloc([KC, 128], BF16, "HOWN")
            U = AR.alloc([12, 131], F32, "U")
            Y = AR.alloc([12, 128], F32, "Y")
            class _Alias:
                def __init__(self, buf, ap):
                    self.buf, self.ap = buf, ap

                def __getitem__(self, idx):
                    return V(self.buf, self.ap[idx])
            SCR = _Alias(Y, Y.t[:, 0:8, :].rearrange("p a n -> p (a n)"))
            SQ = AR.alloc([8, 128], BF16, "SQ")
            RN = AR.alloc([8, 128], F32, "RN")
            QT = AR.alloc([4, 128], BF16, "QT")
            KT_ = AR.alloc([4, 128], BF16, "KTd")
            VT = AR.alloc([4, 128], BF16, "VT")
            TG = AR.alloc([4, 128], BF16, "TG")
            TG2 = AR.alloc([4, 128], BF16, "TG2")
            GHb = AR.alloc([4], BF16, "GHb")
            GLb = AR.alloc([4], BF16, "GLb")
            GHf = AR.alloc([4], F32, "GHf")
            GLf = AR.alloc([4], F32, "GLf")
            GRf = AR.alloc([4], F32, "GRf")
            EGC = AR.alloc([4, 128], F32, "EGC")
            DT = AR.alloc([4, 128], F32, "DT")
            DST = AR.alloc([4, 128], BF16, "DST")
            KTOK = AR.alloc([4, 128], BF16, "KTOK")
            VTOK = AR.alloc([4, 128], BF16, "VTOK")
            ATT = AR.alloc([4, 128], BF16, "ATT")
            QD = AR.alloc([4, 128], BF16, "QD")
            WW = [AR.alloc([4, 3, 128], BF16, "WW%d" % i) for i in range(2)]
            QF = AR.alloc([4, 128], BF16, "QF")
            O1N = AR.alloc([4, 128], BF16, "O1N")
            O2N = AR.alloc([4, 128], BF16, "O2N")
            DINV = AR.alloc([4, 128], BF16, "DINV")
            Y1 = AR.alloc([4, 128], BF16, "Y1")
            D2T = AR.alloc([4, 128], BF16, "D2T")
            TT = AR.alloc([4, 128], BF16, "TT")
            SF = AR.alloc([4, 128], F32, "SF")
            SB = AR.alloc([4, 128], BF16, "SB")
            XB = AR.alloc([4, 128], BF16, "XB")
            VN = AR.alloc([4, 128], BF16, "VN")
            VND = AR.alloc([4, 128], BF16, "VND")
            OACC = AR.alloc([4, 128], F32, "OACC")
            OSQ = AR.alloc([4, 128], BF16, "OSQ")
            RSTD = AR.alloc([4, 128], F32, "RSTD")
            SZ = AR.alloc([4, 128], F32, "SZ")
            RR = AR.alloc([8], F32, "RR")
            TMP = AR.alloc([40], F32, "TMP")
            SM = AR.alloc([40], F32, "SM")
            GC, NGC, BETA, NBETA, EG, NEG_EG, BKD, T1 = (SM[:, i * 4:(i + 1) * 4] for i in range(8))
            CW = PAR[:, 16:64]
            CTMP = AR.alloc([128], F32, "CTMP")
            HTMP = AR.alloc([KC, 128], BF16, "HTMP")

            P.memset("dve", U[:, :, :], 0.0)
            P.memset("dve", SF[:, :, :], 0.0)
            P.memset("dve", SB[:, :, :], 0.0)

            KT_2 = [KT_, AR.alloc([4, 128], BF16, "KTd1")]
            VTOK2 = [VTOK, AR.alloc([4, 128], BF16, "VTOK1")]
            KTOK2 = [KTOK, AR.alloc([4, 128], BF16, "KTOK1")]
            ATT2 = [ATT, AR.alloc([4, 128], BF16, "ATT1")]
            QD2 = [QD, AR.alloc([4, 128], BF16, "QD1")]
            EGC2 = [EGC, AR.alloc([4, 128], F32, "EGC1")]
            SM2 = [SM, AR.alloc([40], F32, "SM1")]
            O1N2 = [O1N, AR.alloc([4, 128], BF16, "O1N1")]
            O2N2 = [O2N, AR.alloc([4, 128], BF16, "O2N1")]
            WI2 = [AR.alloc([4, 3, 128], BF16, "WI%d" % i) for i in range(2)]

            def front(blk):
                par_ = blk % 2
                hn = HN[par_]
                KT_, VTOK, KTOK, ATT, QD, EGC, SM = KT_2[par_], VTOK2[par_], KTOK2[par_], ATT2[par_], QD2[par_], EGC2[par_], SM2[par_]
                O1N, O2N, W0 = O1N2[par_], O2N2[par_], WI2[par_]
                GC, NGC, BETA, NBETA, EG, NEG_EG, BKD, T1 = (SM[:, i * 4:(i + 1) * 4] for i in range(8))
                norm_transpose(xs[blk * 128:(blk + 1) * 128, :], XT[par_], XN, SCR, ST[par_],
                               lambda kc: hn[:, kc, :], NW1, blk, HN_all=hn[:, :, :])
                yield
                if blk > 0:
                    P.copy("act", U[:, :, 0:3], U[:, :, 128:131])
                for g4 in range(3):
                    bk = psum()
                    for ti in range(4):
                        tile_ = g4 * 4 + ti
                        for kc in range(KC):
                            P.mm(V(bk, bk.t[:, ti * 128:(ti + 1) * 128]),
                                 WDN[:, kc, tile_ * 128:(tile_ + 1) * 128], hn[:, kc, :],
                                 start=(kc == 0), stop=(kc == KC - 1))
                    P.copy("act", U[:, g4 * 4:(g4 + 1) * 4, 3:131], pf(bk, 512, 4))
                    yield
                bl = psum()
                for kc in range(KC):
                    P.mm(V(bl, bl.t[:, 0:8]), hn[:, kc, :], WLG[:, kc, :], start=(kc == 0), stop=(kc == KC - 1))
                softplus_chain(V(bl, bl.t[:, 0:8]), 8, PAR[:, 67:75], PAR[:, 75:83], NCO[:, 0:8], RR[:, 0:8], TMP)
                yield
                cw3 = CW.ap.rearrange("p (t i) -> p t i", i=4)
                T4 = RN[:, 0:4, :]
                for g4 in range(3):
                    Yg = Y[:, g4 * 4:(g4 + 1) * 4, :]
                    wb_ = lambda i: V(PAR, cw3[:, g4 * 4:(g4 + 1) * 4, i:i + 1].to_broadcast([128, 4, 128]))
                    P.tt("dve", Yg, U[:, g4 * 4:(g4 + 1) * 4, 0:128], wb_(0), ALU.mult)
                    for i in range(1, 4):
                        P.tt("dve", T4, U[:, g4 * 4:(g4 + 1) * 4, i:i + 128], wb_(i), ALU.mult)
                        P.tt("dve", Yg, Yg, T4, ALU.add)
                    yield
                P.act(Y[:, :, :], Y[:, :, :], AF.Silu)
                P.act(SQ[:, :, :], Y[:, 0:8, :], AF.Square)
                for hf in range(2):
                    bk = psum()
                    P.mm(pf(bk), ONES_b, V(SQ, SQ.t[:, hf * 4:(hf + 1) * 4, :].rearrange("p a n -> p (a n)")))
                    P.act(RN[:, hf * 4:(hf + 1) * 4, :], pf(bk, 512, 4), AF.Ln, bias=c_eps)
                    P.act(RN[:, hf * 4:(hf + 1) * 4, :], RN[:, hf * 4:(hf + 1) * 4, :], AF.Exp, scale=-0.5,
                          bias=(c_lnq if hf == 0 else None))
                yield
                P.tt("dve", QT[:, :, :], Y[:, 0:4, :], RN[:, 0:4, :], ALU.mult)
                P.tt("dve", KT_[:, :, :], Y[:, 4:8, :], RN[:, 4:8, :], ALU.mult)
                P.copy("act", VT[:, :, :], Y[:, 8:12, :])
                yield
                P.copy("dve", GHb[:, :], RR[:, 4:8])
                P.copy("dve", GHf[:, :], GHb[:, :])
                P.tt("dve", GRf[:, :], RR[:, 4:8], GHf[:, :], ALU.subtract)
                P.copy("dve", GLb[:, :], GRf[:, :])
                P.copy("dve", GLf[:, :], GLb[:, :])
                bg = psum()
                P.mm(V(bg, bg.t[:, 0:4]), TRI_b, GHb[:, :], start=True, stop=False)
                P.mm(V(bg, bg.t[:, 0:4]), TRI_b, GLb[:, :], start=False, stop=True)
                P.copy("dve", GC, V(bg, bg.t[:, 0:4]))
                P.ts("dve", NGC, GC, -1.0, ALU.mult)
                yield
                trib = V(CST, TRI.ap.unsqueeze(1).to_broadcast([128, 4, 128]))
                P.tt("dve", TG[:, :, :], trib, V(GHf, GHf.t[:, :].unsqueeze(2).to_broadcast([128, 4, 128])), ALU.mult)
                P.tt("dve", TG2[:, :, :], trib, V(GLf, GLf.t[:, :].unsqueeze(2).to_broadcast([128, 4, 128])), ALU.mult)
                tgf = V(TG, TG.t.rearrange("p a n -> p (a n)"))
                tg2f = V(TG2, TG2.t.rearrange("p a n -> p (a n)"))
                bx = psum()
                P.mm(pf(bx), ONES_b, tgf, start=True, stop=False)
                P.mm(pf(bx), ONES_b, tg2f, start=False, stop=True)
                P.act(EGC[:, :, :], pf(bx, 512, 4), AF.Exp)
                by = psum()
                P.mm(pf(by), ONES_b, tgf, start=True, stop=False)
                P.mm(pf(by), ONES_b, tg2f, start=False, stop=False)
                P.mm(pf(by), I_b, MSK4_b, start=False, stop=True)
                P.tt("dve", T1, RR[:, 0:4], GC, ALU.subtract)
                bxl = V(bx, pf(bx, 512, 4).ap[:, :, 127])
                P.emit("dve", lambda e: e.tensor_tensor(T1.ap, T1.ap, bxl.ap, ALU.add), outs=[T1], ins=[T1, bxl, EGC[:, :, :]])
                for h in range(4):
                    P.act(DT[:, h, :], V(by, by.t[:, h * 128:(h + 1) * 128]), AF.Exp, bias=V(SM, NGC.ap[:, h:h + 1]))
                yield
                P.tt("pool", DST[:, :, :], DT[:, :, :], NOTI4[:, :, :], ALU.mult)
                P.act(BETA, RR[:, 0:4], AF.Exp)
                P.ts("dve", NBETA, BETA, -1.0, ALU.mult)
                P.act(EG, GC, AF.Exp)
                P.ts("dve", NEG_EG, EG, -1.0, ALU.mult)
                P.act(BKD, T1, AF.Exp)
                P.tt("dve", QD[:, :, :], QT[:, :, :], EGC[:, :, :], ALU.mult)
                yield
                bt = psum()
                for h in range(4):
                    P.tr(V(bt, pb(bt).ap[:, h * 128:(h + 1) * 128]), KT_[:, h, :], I_b)
                    P.tr(V(bt, pb(bt).ap[:, 512 + h * 128:512 + (h + 1) * 128]), VT[:, h, :], I_b)
                P.copy("dve", KTOK[:, :, :], V(bt, pb(bt).ap[:, 0:512].rearrange("p (h n) -> p h n", h=4)))
                P.copy("dve", V(VTOK, VTOK.t.rearrange("p h n -> p (h n)")), V(bt, pb(bt).ap[:, 512:1024]))
                yield
                bkk = psum()
                bqk = psum()
                for h in range(4):
                    P.mm(V(bkk, bkk.t[:, h * 128:(h + 1) * 128]), KT_[:, h, :], KT_[:, h, :])
                    P.mm(V(bqk, bqk.t[:, h * 128:(h + 1) * 128]), KT_[:, h, :], QT[:, h, :])
                P.tt("dve", ATT[:, :, :], pf(bqk, 512, 4), DT[:, :, :], ALU.mult)
                for h in range(4):
                    P.stt("dve", QF[:, h, :], V(bkk, bkk.t[:, h * 128:(h + 1) * 128]), V(SM, NBETA.ap[:, h:h + 1]),
                          DST[:, h, :], ALU.mult, ALU.mult)
                P.copy("dve", W0[:, :, 0, :], V(CB, I_b.ap.unsqueeze(1).to_broadcast([128, 4, 128])))
                P.tt("dve", W0[:, :, 1, :], QF[:, :, :], BD4[:, :, :], ALU.mult)
                yield
                bq = psum()
                for h in range(4):
                    P.tr(V(bq, pb(bq).ap[:, h * 128:(h + 1) * 128]), QF[:, h, :], I_b)
                bq4 = V(bq, pb(bq).ap[:, 0:512].rearrange("p (h n) -> p h n", h=4))
                P.tt("dve", W0[:, :, 2, :], bq4, BD4[:, :, :], ALU.mult)
                P.tt("dve", O1N[:, :, :], bq4, L14[:, :, :], ALU.mult)
                P.tt("dve", O2N[:, :, :], bq4, L24[:, :, :], ALU.mult)
                yield

            def chain(blk):
                par_ = blk % 2
                hn = HN[par_]
                KT_, VTOK, KTOK, ATT, QD, EGC, SM = KT_2[par_], VTOK2[par_], KTOK2[par_], ATT2[par_], QD2[par_], EGC2[par_], SM2[par_]
                O1N, O2N = O1N2[par_], O2N2[par_]
                GC, NGC, BETA, NBETA, EG, NEG_EG, BKD, T1 = (SM[:, i * 4:(i + 1) * 4] for i in range(8))
                NR = 5
                for r in range(NR):
                    Wc = WI2[par_] if r == 0 else WW[(r - 1) % 2]
                    Wn = WW[r % 2]
                    last = (r == NR - 1)
                    for h in range(4):
                        bk = psum()
                        if not last:
                            P.mm(V(bk, bk.t[:, 0:256]), Wc[:, h, 2, :], V(Wc, Wc.t[:, h, 0:2, :].rearrange("p a n -> p (a n)")))
                            P.mm(V(bk, bk.t[:, 256:384]), Wc[:, h, 1, :], Wc[:, h, 2, :])
                            P.tt("dve", Wn[:, h, 0, :], V(bk, bk.t[:, 0:128]), Wc[:, h, 0, :], ALU.add)
                            P.copy("act", Wn[:, h, 1:3, :], V(bk, bk.t[:, 128:384].rearrange("p (a n) -> p a n", a=2)))
                        else:
                            P.mm(V(bk, bk.t[:, 0:128]), Wc[:, h, 2, :], Wc[:, h, 0, :])
                            P.tt("dve", Wn[:, h, 0, :], V(bk, bk.t[:, 0:128]), Wc[:, h, 0, :], ALU.add)
                    yield
                WD_ = WW[(NR - 1) % 2]
                prev = [WD_[:, h, 0, :] for h in range(4)]
                prev4 = WD_[:, :, 0, :]
                for lvl, (ON, DNX, YY, OUT4) in enumerate(((O1N, DINV, Y1, D2T), (O2N, DINV, Y1, TT))):
                    btr = psum()
                    for h in range(4):
                        P.tr(V(btr, pb(btr).ap[:, h * 128:(h + 1) * 128]), prev[h], I_b)
                    P.copy("dve", DNX[:, :, :], V(btr, pb(btr).ap[:, 0:512].rearrange("p (h n) -> p h n", h=4)))
                    by_ = psum()
                    for h in range(4):
                        P.mm(V(by_, by_.t[:, h * 128:(h + 1) * 128]), ON[:, h, :], prev[h])
                    P.copy("act", YY[:, :, :], pf(by_, 512, 4))
                    yield
                    bp_ = psum()
                    for h in range(4):
                        P.mm(V(bp_, bp_.t[:, h * 128:(h + 1) * 128]), DNX[:, h, :], YY[:, h, :])
                    P.tt("dve", OUT4[:, :, :], pf(bp_, 512, 4), prev4, ALU.add)
                    prev = [OUT4[:, h, :] for h in range(4)]
                    prev4 = OUT4[:, :, :]
                    yield
                b1 = psum()
                for h in range(4):
                    P.mm(V(b1, b1.t[:, h * 128:(h + 1) * 128]), KT_[:, h, :], SB[:, h, :])
                bc4 = lambda v_: V(v_.buf, v_.ap.unsqueeze(2).to_broadcast([128, 4, 128]))
                P.tt("dve", RSTD[:, :, :], pf(b1, 512, 4), bc4(NEG_EG), ALU.mult)
                P.tt("dve", XB[:, :, :], RSTD[:, :, :], VTOK[:, :, :], ALU.add)
                yield
                b2 = psum()
                for h in range(4):
                    P.mm(V(b2, b2.t[:, h * 128:(h + 1) * 128]), TT[:, h, :], XB[:, h, :])
                P.tt("dve", VN[:, :, :], pf(b2, 512, 4), bc4(BETA), ALU.mult)
                P.tt("dve", VND[:, :, :], pf(b2, 512, 4), bc4(BKD), ALU.mult)
                yield
                if blk > 0:
                    b3 = psum()
                    for h in range(4):
                        o_ = V(b3, b3.t[:, h * 128:(h + 1) * 128])
                        P.mm(o_, SB[:, h, :], QD[:, h, :], start=True, stop=False)
                        P.mm(o_, VN[:, h, :], ATT[:, h, :], start=False, stop=True)
                    if blk % 2 == 1:
                        P.ts("dve", OACC[:, :, :], pf(b3, 512, 4), mA, ALU.mult)
                    else:
                        P.stt("dve", OACC[:, :, :], pf(b3, 512, 4), mB, OACC[:, :, :], ALU.mult, ALU.add)
                b4 = psum()
                for h in range(4):
                    P.mm(V(b4, b4.t[:, h * 128:(h + 1) * 128]), KTOK[:, h, :], VND[:, h, :])
                cdb = V(EGC, EGC.t[:, :, 127:128].to_broadcast([128, 4, 128]))
                P.tt("dve", SF[:, :, :], SF[:, :, :], cdb, ALU.mult)
                P.tt("dve", SF[:, :, :], SF[:, :, :], pf(b4, 512, 4), ALU.add)
                P.copy("pool", SB[:, :, :], SF[:, :, :])
                yield
                if blk > 0:
                    if blk % 2 == 1:
                        P.ts("pool", HOWN[:, :, :], hn[:, :, :], mA, ALU.mult)
                    else:
                        P.ts("pool", HTMP[:, :, :], hn[:, :, :], mB, ALU.mult)
                        P.tt("pool", HOWN[:, :, :], HOWN[:, :, :], HTMP[:, :, :], ALU.add)
                        p = blk // 2 - 1
                        bz = psum()
                        for ti in range(4):
                            for kc in range(KC):
                                P.mm(V(bz, bz.t[:, ti * 128:(ti + 1) * 128]),
                                     WDN[:, kc, 1536 + ti * 128:1536 + (ti + 1) * 128], HOWN[:, kc, :],
                                     start=(kc == 0), stop=(kc == KC - 1))
                        P.act(SZ[:, :, :], pf(bz, 512, 4), AF.Silu)
                        yield
                        P.act(OSQ[:, :, :], OACC[:, :, :], AF.Square)
                        b5 = psum()
                        P.mm(pf(b5), ONES_b, V(OSQ, OSQ.t.rearrange("p a n -> p (a n)")))
                        P.act(RSTD[:, :, :], pf(b5, 512, 4), AF.Ln, scale=1.0 / 128, bias=c_eps)
                        P.act(RSTD[:, :, :], RSTD[:, :, :], AF.Exp, scale=-0.5)
                        P.stt("dve", RSTD[:, :, :], OACC[:, :, :], PAR[:, 64:65], RSTD[:, :, :], ALU.mult, ALU.mult)
                        P.tt("dve", ODN[p][:, :, :], RSTD[:, :, :], SZ[:, :, :], ALU.mult)
                yield

            def interleave(gens):
                gens = list(gens)
                while gens:
                    for g in list(gens):
                        try:
                            next(g)
                        except StopIteration:
                            gens.remove(g)

            interleave([front(0)])
            for blk in range(nb1):
                gl = [chain(blk)]
                if blk + 1 < nb1:
                    gl.append(front(blk + 1))
                if PIPELINE1:
                    interleave(gl)
                else:
                    for g in gl:
                        interleave([g])

            if dbg and upto == 1:
                DB = Buf(RSTD.t.rearrange("p h n -> p (h n)"), "DB")
                P.dma("sp", Dr(dbg_o[:, 15 * 512:16 * 512]), V(SF, SF.t.rearrange("p h n -> p (h n)")))
                extras = [(10, QT, None), (11, KT_2[0], None), (12, VT, None), (13, ATT2[0], None), (14, TT, None)]
                for slot, bufx, sub in extras:
                    if stage < 6:
                        break
                    src = bufx.t.rearrange("p h n -> p (h n)") if sub is None else None
                    if sub is None:
                        P.copy("dve", DB[:, :], V(bufx, src))
                    else:
                        for h in range(4):
                            P.copy("dve", DB[:, h * 128:(h + 1) * 128], bufx[:, h, sub, :])
                    P.dma("sp", Dr(dbg_o[:, slot * 512:(slot + 1) * 512]), DB[:, :])
                for p in range(min(10, (nb1 - 1) // 2) if stage >= 6 else 0):
                    P.copy("dve", DB[:, :], V(ODN[p], ODN[p].t.rearrange("p h n -> p (h n)")))
                    P.dma("sp", Dr(dbg_o[:, p * 512:(p + 1) * 512]), DB[:, :])

        if upto >= 2:
            P.barrier()
            AR.lo = 0
            bstate["n"] = 6
            OBK = [banks[6], banks[7]]
            WFX = AR.alloc([KC, 1024], BF16, "WFX")
            WFV = AR.alloc([KC, 512], BF16, "WFV")
            WLF = AR.alloc([KC, 8], BF16, "WLF")
            if upto >= 3 and not skip1:
                P.dma("sp", WFX[:, :, :], S_WFX[:, :, :])
                P.dma("sp", WFV[:, :, :], S_WFV[:, :, :])
                P.dma("sp", WLF[:, :, :], S_WLF[:, :, :])
            else:
                for kc in range(KC):
                    P.dma("pool", WFX[:, kc, 0:512], Dr(w_in[kc * 128:(kc + 1) * 128, 512:1024]))
                    P.dma("pool", WFX[:, kc, 512:1024], Dr(w_in[kc * 128:(kc + 1) * 128, 0:512]))
                    P.dma("pool", WFV[:, kc, :], Dr(w_in[kc * 128:(kc + 1) * 128, 1024:1536]))
                    P.dma("pool", WLF[:, kc, :], Dr(w_in[kc * 128:(kc + 1) * 128, 1536:1544]))
            KT = [AR.alloc([4, 128], BF16, "KT%d" % b) for b in range(NB)]
            VA = [AR.alloc([8, 65], BF16, "VA%d" % b) for b in range(NB)]
            XT = [AR.alloc([D], F32, "XT%d" % i) for i in range(2)]
            XN = AR.alloc([D], BF16, "XN")
            SCR = AR.alloc([D], BF16, "SCR")
            ST = [AR.alloc([4], F32, "ST%d" % i) for i in range(2)]
            HN = [AR.alloc([KC, 128], BF16, "HN%d" % i) for i in range(2)]
            HOWN = AR.alloc([KC, 128], BF16, "HOWN")
            QTP = AR.alloc([4, 128], BF16, "QTP")
            HTMP = AR.alloc([KC, 128], BF16, "HTMP")
            NEGC = AR.alloc([8, NB], F32, "NEGC")
            CREF = AR.alloc([8, NB], F32, "CREF")
            CP = [AR.alloc([8], F32, "CP%d" % i) for i in range(2)]
            LF = AR.alloc([8], F32, "LF")
            CP3 = AR.alloc([3, 8], BF16, "CP3")
            LF3 = AR.alloc([3, 8], BF16, "LF3")
            SPT = AR.alloc([3, 8], F32, "SPT")
            TMP = AR.alloc([40], F32, "TMP")
            BT = [AR.alloc([NB], F32, "BT%d" % i) for i in range(2)]
            PT = [AR.alloc([512], BF16, "PT%d" % i) for i in range(4)]
            OT = AR.alloc([512], BF16, "OT")
            RS = AR.alloc([8], F32, "RS")
            MSKA = CB[:, 256:384]
            MSKB = CB[:, 384:512]
            P.memset("dve", CP[1][:, :], 0.0)
            for b in range(NB):
                P.memset("pool", VA[b][:, :, 64:65], 1.0)
            ptc = [0]

            QTP2 = [QTP, AR.alloc([4, 128], BF16, "QTP1")]
            OBH = [Buf(banks[6 + (h // 4) % 2].t[:, (h % 4) * 65:(h % 4 + 1) * 65], "OBH%d" % h) for h in range(8)]

            pools["S"] = ([0, 1, 2, 3], [0])
            pools["J"] = ([4, 5], [0])

            def proj(blk):
                par_ = blk % 2
                hn = HN[par_]
                norm_transpose(xs[blk * 128:(blk + 1) * 128, :], XT[par_], XN, SCR, ST[par_],
                               lambda kc: hn[:, kc, :], NW1, blk, pool="J", part=0)
                yield
                norm_transpose(xs[blk * 128:(blk + 1) * 128, :], XT[par_], XN, SCR, ST[par_],
                               lambda kc: hn[:, kc, :], NW1, blk, pool="J", part=1, HN_all=hn[:, :, :])
                yield
                bk = psum("J")
                for ti in range(4):
                    for kc in range(KC):
                        P.mm(V(bk, bk.t[:, ti * 128:(ti + 1) * 128]), WFX[:, kc, ti * 128:(ti + 1) * 128], hn[:, kc, :],
                             start=(kc == 0), stop=(kc == KC - 1))
                P.copy("dve", KT[blk][:, :, :], pf(bk, 512, 4))
                yield
                bv = psum("J")
                for kc in range(KC):
                    P.mm(pf(bv), hn[:, kc, :], WFV[:, kc, :], start=(kc == 0), stop=(kc == KC - 1))
                P.copy("dve", VA[blk][:, :, 0:64], pf(bv, 512, 8))
                yield
                bl = psum("J")
                for kc in range(KC):
                    P.mm(V(bl, bl.t[:, 0:8]), hn[:, kc, :], WLF[:, kc, :], start=(kc == 0), stop=(kc == KC - 1))
                softplus_chain(V(bl, bl.t[:, 0:8]), 8, PAR[:, 87:95], NCO[:, 8:16], NCO[:, 8:16], LF[:, 0:8], TMP)
                yield
                cprev = CP[1 - par_]
                ccur = CP[par_]
                split3(CP3, cprev[:, :], SPT)
                split3(LF3, LF[:, :], SPT)
                bc = psum("J")
                for t in range(3):
                    P.mm(V(bc, bc.t[:, 0:8]), E127_b, CP3[:, t, :], start=(t == 0), stop=(t == 2))
                for t in range(3):
                    P.mm(V(bc, bc.t[:, 8:16]), TRI_b, LF3[:, t, :], start=(t == 0), stop=False)
                for t in range(3):
                    P.mm(V(bc, bc.t[:, 8:16]), E127_b, CP3[:, t, :], start=False, stop=(t == 2))
                P.copy("dve", CREF[:, :, blk], V(bc, bc.t[:, 0:8]))
                P.copy("dve", ccur[:, :], V(bc, bc.t[:, 8:16]))
                if blk == 0:
                    P.ts("dve", NEGC[:, :, blk], V(bc, bc.t[:, 8:16]), -1.0, ALU.mult, c_padm, ALU.add)
                else:
                    P.ts("dve", NEGC[:, :, blk], V(bc, bc.t[:, 8:16]), -1.0, ALU.mult)
                yield
                if blk == 0:
                    return
                if blk % 2 == 1:
                    P.ts("pool", HOWN[:, :, :], hn[:, :, :], mA, ALU.mult)
                    yield
                    return
                P.ts("pool", HTMP[:, :, :], hn[:, :, :], mB, ALU.mult)
                P.tt("pool", HOWN[:, :, :], HOWN[:, :, :], HTMP[:, :, :], ALU.add)
                p = blk // 2 - 1
                qtp = QTP2[p % 2]
                bqp = psum("J")
                for ti in range(4):
                    for kc in range(KC):
                        P.mm(V(bqp, bqp.t[:, ti * 128:(ti + 1) * 128]), WFX[:, kc, 512 + ti * 128:512 + (ti + 1) * 128],
                             HOWN[:, kc, :], start=(kc == 0), stop=(kc == KC - 1))
                P.copy("dve", qtp[:, :, :], pf(bqp, 512, 4))
                yield

            def attn(p):
                blk = 2 * p + 2
                blkA = blk - 1
                qtp = QTP2[p % 2]
                nk = blk + 1
                groups = [list(range(g, min(g + 4, nk))) for g in range(0, nk, 4)]
                for h in range(8):
                    hp, base = h // 2, (h % 2) * 64
                    ob = OBH[h]
                    bt_ = BT[h % 2]
                    P.ts("dve", bt_[:, 0:nk], NEGC[:, h, 0:nk], CREF[:, h, blkA:blkA + 1], ALU.add)

                    def s_group(kbs):
                        bs = psum("S")
                        for i, kb in enumerate(kbs):
                            o_ = V(bs, bs.t[:, i * 128:(i + 1) * 128])
                            masked = kb >= blkA
                            P.mm(o_, KT[kb][base:base + 64, hp, :], qtp[base:base + 64, hp, :], start=True, stop=not masked)
                            if masked:
                                P.mm(o_, I_b, MSKA if kb == blkA else MSKB, start=False, stop=True)
                        return bs

                    ng = len(groups)
                    bq_ = [s_group(groups[i]) for i in range(min(SDEPTH, ng))]
                    for gi, kbs in enumerate(groups):
                        bs = bq_.pop(0)
                        if gi + SDEPTH < ng:
                            bq_.append(s_group(groups[gi + SDEPTH]))
                        pt = PT[ptc[0] % 4]
                        ptc[0] += 1
                        for i, kb in enumerate(kbs):
                            P.act(pt[:, i * 128:(i + 1) * 128], V(bs, bs.t[:, i * 128:(i + 1) * 128]), AF.Exp,
                                  bias=bt_[:, kb:kb + 1], scale=0.125)
                        for i, kb in enumerate(kbs):
                            P.mm(ob[:, 0:65], pt[:, i * 128:(i + 1) * 128], VA[kb][:, h, :],
                                 start=(kb == 0), stop=(kb == nk - 1))
                        if gi % 2 == 1:
                            yield
                    P.recip(RS[:, h:h + 1], ob[:, 64:65])
                    P.ts("dve", OT[:, h * 64:(h + 1) * 64], ob[:, 0:64], RS[:, h:h + 1], ALU.mult)
                    yield
                bo = psum("J")
                for c4 in range(4):
                    P.tr(V(bo, pb(bo).ap[:, c4 * 128:(c4 + 1) * 128]), OT[:, c4 * 128:(c4 + 1) * 128], I_b)
                P.copy("dve", OFX[p][:, :, :], V(bo, pb(bo).ap[:, 0:512].rearrange("p (h n) -> p h n", h=4)))
                yield

            def seq(*gs):
                for g in gs:
                    yield from g

            def interleave2(gens):
                gens = list(gens)
                while gens:
                    for g in list(gens):
                        try:
                            next(g)
                        except StopIteration:
                            gens.remove(g)

            interleave2([seq(proj(0), proj(1), proj(2))])
            for p in range(NP):
                gl = [attn(p)]
                nxt = [proj(b) for b in (2 * p + 3, 2 * p + 4) if b < NB]
                if nxt:
                    gl.append(seq(*nxt))
                if PIPELINE2:
                    interleave2(gl)
                else:
                    for g in gl:
                        interleave2([g])
            bstate["n"] = 8

            if dbg and upto == 2:
                DB = AR.alloc([512], F32, "DB")
                for p in range(NP):
                    P.copy("dve", DB[:, :], V(OFX[p], OFX[p].t.rearrange("p h n -> p (h n)")))
                    P.dma("sp", Dr(dbg_o[:, p * 512:(p + 1) * 512]), DB[:, :])

        if upto >= 3:
            P.barrier()
            AR.lo = 0
            HNA = AR.alloc([KC, NP * 128], BF16, "HNA")
            YT = AR.alloc([KC, NP * 128], BF16, "YT")
            WO = AR.alloc([KC, D], BF16, "WO")
            WG2 = [AR.alloc([KC, 256], BF16, "WG2_%d" % i) for i in range(2)]
            WB2 = [AR.alloc([4, 256], BF16, "WB2_%d" % i) for i in range(2)]
            XT = [AR.alloc([D], F32, "XT%d" % i) for i in range(2)]
            XN = AR.alloc([D], BF16, "XN")
            SCR = AR.alloc([D], BF16, "SCR")
            ST = [AR.alloc([4], F32, "ST%d" % i) for i in range(2)]
            SGA = AR.alloc([512], F32, "SGA")
            SGB = AR.alloc([512], F32, "SGB")
            T1_ = AR.alloc([512], F32, "T1_")
            T2_ = AR.alloc([512], F32, "T2_")
            P.dma("sp", WO[:, :, :], S_WO[:, :, :])
            for j in range(NP):
                norm_transpose(xo[j * 128:(j + 1) * 128, :], XT[j % 2], XN, SCR, ST[j % 2],
                               lambda kc: HNA[:, kc, j * 128:(j + 1) * 128], NW1, j, HN_all=HNA[:, :, j * 128:(j + 1) * 128])
            for nt in range(8):
                wg = WG2[nt % 2]
                wb = WB2[nt % 2]
                P.dma("sp", wg[:, :, :], S_GAB[nt][:, :, :])
                P.dma("sp", wb[:, :, :], S_BFD[nt][:, :, :])
                for tt_ in range(4):
                    bga, bgb, byf, byd = psum(), psum(), psum(), psum()
                    for kc in range(KC):
                        P.mm(pf(bga), wg[:, kc, 0:128], HNA[:, kc, tt_ * 512:(tt_ + 1) * 512], start=(kc == 0), stop=(kc == KC - 1))
                    for kc in range(KC):
                        P.mm(pf(bgb), wg[:, kc, 128:256], HNA[:, kc, tt_ * 512:(tt_ + 1) * 512], start=(kc == 0), stop=(kc == KC - 1))
                    for q4 in range(4):
                        for kc in range(4):
                            pidx = tt_ * 4 + q4
                            P.mm(V(byf, byf.t[:, q4 * 128:(q4 + 1) * 128]), wb[:, kc, 0:128], OFX[pidx][:, kc, :],
                                 start=(kc == 0), stop=(kc == 3))
                    for q4 in range(4):
                        for kc in range(4):
                            pidx = tt_ * 4 + q4
                            P.mm(V(byd, byd.t[:, q4 * 128:(q4 + 1) * 128]), wb[:, kc, 128:256], ODN[pidx][:, kc, :],
                                 start=(kc == 0), stop=(kc == 3))
                    P.act(SGA[:, :], pf(bga), AF.Sigmoid)
                    P.act(SGB[:, :], pf(bgb), AF.Sigmoid)
                    P.tt("dve", T1_[:, :], SGA[:, :], pf(byf), ALU.mult)
                    P.tt("dve", T2_[:, :], SGB[:, :], pf(byd), ALU.mult)
                    P.tt("pool", YT[:, nt, tt_ * 512:(tt_ + 1) * 512], T1_[:, :], T2_[:, :], ALU.add)
            for j in range(NP):
                for mh in range(2):
                    bm = psum()
                    for kc in range(KC):
                        P.mm(pf(bm), YT[:, kc, j * 128:(j + 1) * 128], WO[:, kc, mh * 512:(mh + 1) * 512],
                             start=(kc == 0), stop=(kc == KC - 1))
                    P.copy("act" if mh else "dve", MIX[:, j, mh * 512:(mh + 1) * 512], pf(bm))

            if dbg and upto == 3:
                pass
            P.barrier()
            AR.lo = 0
            AR.hi = mix_mark
            ACTT = AR.alloc([NFT, 1024], BF16, "ACTT")
            WD = AR.alloc([NFT, D], BF16, "WD")
            H1N = AR.alloc([KC, 1024], BF16, "H1N")
            WGU = [AR.alloc([KC, 256], BF16, "WGU%d" % i) for i in range(3)]
            XT = [AR.alloc([D], F32, "XT%d" % i) for i in range(2)]
            H1 = AR.alloc([D], F32, "H1")
            H2 = AR.alloc([D], F32, "H2")
            OUT = [AR.alloc([D], F32, "OUT%d" % i) for i in range(2)]
            XN = AR.alloc([D], BF16, "XN")
            SCR = AR.alloc([D], BF16, "SCR")
            ST = [AR.alloc([4], F32, "ST%d" % i) for i in range(2)]
            SG = [AR.alloc([512], F32, "SG%d" % i) for i in range(2)]
            P.dma("sp", WD[:, 0:11, :], S_WD[:, 0:11, :])
            P.dma("sp", WD[:, 11:22, :], S_WD[:, 11:22, :])
            for t2 in range(2):
                for jl in range(8):
                    j = t2 * 8 + jl
                    xt = XT[j % 2]
                    st = ST[j % 2]
                    P.dma("sp", xt[:, :], Dr(xo[j * 128:(j + 1) * 128, :]))
                    P.tt("dve", H1[:, :], xt[:, :], MIX[:, j, :], ALU.add)
                    P.act(SCR[:, :], H1[:, :], AF.Square, accum=st[:, 0:1])
                    P.act(st[:, 1:2], st[:, 0:1], AF.Ln, scale=1.0 / D, bias=c_eps)
                    P.act(st[:, 2:3], st[:, 1:2], AF.Exp, scale=-0.5)
                    P.ts("dve", XN[:, :], H1[:, :], st[:, 2:3], ALU.mult)
                    bk = psum()
                    for kc in range(KC):
                        P.tr(V(bk, pb(bk).ap[:, kc * 128:(kc + 1) * 128]), XN[:, kc * 128:(kc + 1) * 128], I_b)
                    for kc in range(KC):
                        src = V(bk, pb(bk).ap[:, kc * 128:(kc + 1) * 128])
                        if True:
                            P.ts("dve", H1N[:, kc, jl * 128:(jl + 1) * 128], src, V(PAR, NW2.ap[:, kc:kc + 1]), ALU.mult)
                        else:
                            P.act(H1N[:, kc, jl * 128:(jl + 1) * 128], src, AF.Copy, scale=V(PAR, NW2.ap[:, kc:kc + 1]))
                for ft in range(NFT):
                    wgu = WGU[ft % 3]
                    P.dma("sp", wgu[:, :, :], S_GU[ft][:, :, :])
                    for hf in range(2):
                        bg_, bu_ = psum(), psum()
                        for kc in range(KC):
                            P.mm(pf(bg_), wgu[:, kc, 0:128], H1N[:, kc, hf * 512:(hf + 1) * 512], start=(kc == 0), stop=(kc == KC - 1))
                        for kc in range(KC):
                            P.mm(pf(bu_), wgu[:, kc, 128:256], H1N[:, kc, hf * 512:(hf + 1) * 512], start=(kc == 0), stop=(kc == KC - 1))
                        sg = SG[hf]
                        P.act(sg[:, :], pf(bg_), AF.Silu)
                        P.tt("dve", ACTT[:, ft, hf * 512:(hf + 1) * 512], sg[:, :], pf(bu_), ALU.mult)
                for jl in range(8):
                    j = t2 * 8 + jl
                    xt = XT[j % 2]
                    st = ST[j % 2]
                    out_ = OUT[j % 2]
                    P.dma("sp", xt[:, :], Dr(xo[j * 128:(j + 1) * 128, :]))
                    P.tt("pool", H1[:, :], xt[:, :], MIX[:, j, :], ALU.add)
                    for mh in range(2):
                        bd_ = psum()
                        for kc in range(NFT):
                            P.mm(pf(bd_), ACTT[:, kc, jl * 128:(jl + 1) * 128], WD[:, kc, mh * 512:(mh + 1) * 512],
                                 start=(kc == 0), stop=(kc == NFT - 1))
                        P.tt("dve", H2[:, mh * 512:(mh + 1) * 512], H1[:, mh * 512:(mh + 1) * 512], pf(bd_), ALU.add)
                    P.act(SCR[:, :], H2[:, :], AF.Square, accum=st[:, 0:1])
                    P.act(st[:, 1:2], st[:, 0:1], AF.Ln, scale=1.0 / D, bias=c_eps)
                    P.act(st[:, 2:3], st[:, 1:2], AF.Exp, scale=-0.5)
                    P.stt("dve", out_[:, :], H2[:, :], st[:, 2:3], FNW[:, :], ALU.mult, ALU.mult)
                    P.dma("sp", Dr(y[j * 128:(j + 1) * 128, :]), out_[:, :])

        P.run(block)
        print("instr counts", {e: len(P.q[e]) for e in ENGS}, "ndma", P.ndma, flush=True)
    return nc


_NC_CACHE = {}


def _consts():
    c = np.zeros((128, 1296), np.float32)
    idx = np.arange(128)
    c[:, 0:128] = np.eye(128)
    c[:, 128:256] = (idx[:, None] <= idx[None, :])
    c[127, 256:384] = 1.0
    c[:, 384:512] = 1.0
    c[:, 512:640] = np.where(idx[None, :] >= idx[:, None], 0.0, -1e9)
    c[:, 640:768] = 1.0 - np.eye(128)
    b32 = idx // 32
    c[:, 768:896] = (b32[:, None] == b32[None, :])
    c[:, 896:1024] = ((b32[:, None] % 2 == 1) & (b32[None, :] == b32[:, None] - 1))
    c[:, 1024:1152] = ((idx[:, None] >= 64) & (idx[None, :] < 64))
    c[:, 1280] = EPS
    c[:, 1281] = 1.0
    c[:, 1282] = -0.5 * math.log(128.0)
    c[:112, 1283] = NEG
    return c


def make_in_maps(x, meta_tokens, mix_norm_w, w_in, fox_forget_bias, dn_conv_w, dn_a_log, dn_dt_bias,
                 dn_out_norm_w, w_branch_fox, w_branch_dn, w_out, ffn_norm_w, w_ffn_gate, w_ffn_up,
                 w_ffn_down, final_norm_w):
    f = lambda a: np.ascontiguousarray(np.asarray(a, dtype=np.float32))
    x = f(x)
    cst = _consts()
    idx = np.arange(128)
    causal = np.where(idx[:, None] <= idx[None, :], 0.0, NEG).astype(np.float32)
    allm = np.full((128, 128), NEG, np.float32)
    zero = np.zeros((128, 128), np.float32)
    fnw = np.ascontiguousarray(np.broadcast_to(f(final_norm_w)[None, :], (128, D)))
    bc = lambda v: np.broadcast_to(f(v).reshape(1, -1), (128, f(v).size))
    in_maps = []
    for c in range(8):
        b, s = c // 2, c % 2
        xs = np.concatenate([np.zeros((112, D), np.float32), f(meta_tokens), x[b]], axis=0)
        own = [2 * p - 1 + s for p in range(1, NP + 1)]
        xo = np.concatenate([xs[j * 128:(j + 1) * 128] for j in own], axis=0)
        par = np.zeros((128, 96), np.float32)
        par[:, 0:8] = f(mix_norm_w)[0].reshape(8, 128).T
        par[:, 8:16] = f(ffn_norm_w)[0].reshape(8, 128).T
        par[:, 16:64] = f(dn_conv_w)[0].reshape(4, 12, 128).transpose(2, 1, 0).reshape(128, 48)
        par[:, 64] = f(dn_out_norm_w)[0]
        par[:, 65] = 1.0 if s == 0 else 0.0
        par[:, 66] = 0.0 if s == 0 else 1.0
        par[:, 71:75] = bc(dn_dt_bias[0])
        par[:, 75:79] = -1.0
        par[:, 79:83] = 1.0
        par[:, 83:87] = bc(dn_a_log[0])
        par[:, 87:95] = bc(fox_forget_bias[0])
        mskq = np.concatenate([causal, allm] if s == 0 else [zero, causal], axis=1)
        in_maps.append({
            "xs": np.ascontiguousarray(xs), "xo": np.ascontiguousarray(xo), "w_in": f(w_in)[0],
            "w_bf": f(w_branch_fox)[0], "w_bd": f(w_branch_dn)[0], "w_out": f(w_out)[0],
            "w_g": f(w_ffn_gate)[0], "w_u": f(w_ffn_up)[0], "w_d": f(w_ffn_down)[0],
            "cst": cst, "par": par, "mskq": np.ascontiguousarray(mskq), "fnw": fnw,
        })
    return in_maps


def kernel(**inputs):
    if "nc" not in _NC_CACHE:
        _NC_CACHE["nc"] = build()
    nc = _NC_CACHE["nc"]
    in_maps = make_in_maps(**inputs)
    res = run_bass_kernel_spmd(nc, in_maps, core_ids=list(range(8)))
    out = np.zeros((4, 4096, D), np.float32)
    for c in range(8):
        b, s = c // 2, c % 2
        yc = np.asarray(res.results[c]["y"])
        for p in range(1, NP + 1):
            blk = 2 * p - 1 + s
            out[b, (blk - 1) * 128:blk * 128, :] = yc[(p - 1) * 128:p * 128, :]
    return out
```
